# Optimizing a Trainium2 kernel written in Bass

```python
import jax, jax.numpy as jnp
from jax import lax
import numpy as np

D_MODEL = 1024
BATCH = 8
SEQ = 4096
DEPTH = 2

GRID_W = 64
HEAD_DIM = 64
FNET_GROUPS = 4
FNET_WIDTH = FNET_GROUPS * HEAD_DIM
SGU_HEADS = 4
SGU_WIDTH = SGU_HEADS * HEAD_DIM
SGU_CHUNK = 128
N_Q_HEADS = 8
N_KV_HEADS = 2
Q_PER_KV = N_Q_HEADS // N_KV_HEADS
Q_WIDTH = N_Q_HEADS * HEAD_DIM
KV_WIDTH = N_KV_HEADS * HEAD_DIM
Q_BLOCK = 128
ROPE_THETA = 10000.0
ROPE_FREQS = HEAD_DIM // 4
D_MIX = FNET_WIDTH + SGU_WIDTH + Q_WIDTH
D_IN = FNET_WIDTH + 2 * SGU_WIDTH + Q_WIDTH + 2 * KV_WIDTH
D_FF = -(-8 * D_MODEL // (3 * 256)) * 256
EPS = 1e-6

kernel_name = "hybrid_fnet_sgu_gqa_encoder"


def rmsnorm(x, g):
    xf = x.astype(jnp.float32)
    y = xf * lax.rsqrt(jnp.mean(xf * xf, axis=-1, keepdims=True) + EPS)
    return (y * g.astype(jnp.float32)).astype(x.dtype)


def layernorm(x, g):
    xf = x.astype(jnp.float32)
    mu = jnp.mean(xf, axis=-1, keepdims=True)
    var = jnp.mean(jnp.square(xf - mu), axis=-1, keepdims=True)
    y = (xf - mu) * lax.rsqrt(var + EPS)
    return (y * g.astype(jnp.float32)).astype(x.dtype)


def fourier_mix(h):
    b, s, _ = h.shape
    hg = h.reshape(b, s, FNET_GROUPS, HEAD_DIM).astype(jnp.float32)
    y = jnp.fft.fft2(hg, axes=(1, 3), norm="ortho").real
    return y.reshape(b, s, FNET_WIDTH).astype(h.dtype)


def spatial_gating(z, w_s, b_s, g_v):
    b, s, _ = z.shape
    z = jax.nn.gelu(z)
    u, v = z[..., :SGU_WIDTH], z[..., SGU_WIDTH:]
    v = layernorm(v, g_v)
    v = v.reshape(b, s // SGU_CHUNK, SGU_CHUNK, SGU_HEADS, HEAD_DIM)
    sv = jnp.einsum('hpq,bnqhc->bnphc', w_s.astype(v.dtype), v) + b_s.T[:, :, None].astype(v.dtype)
    return u * sv.reshape(b, s, SGU_WIDTH)


def axial_rope_tables(s, dtype):
    rows = s // GRID_W
    row = jnp.repeat(jnp.arange(rows), GRID_W).astype(jnp.float32)
    col = jnp.tile(jnp.arange(GRID_W), rows).astype(jnp.float32)
    freqs = ROPE_THETA ** (-jnp.arange(ROPE_FREQS, dtype=jnp.float32) / ROPE_FREQS)
    ang_r = row[:, None] * freqs
    ang_c = col[:, None] * freqs
    return (jnp.cos(ang_r).astype(dtype), jnp.sin(ang_r).astype(dtype),
            jnp.cos(ang_c).astype(dtype), jnp.sin(ang_c).astype(dtype))


def rope_half(x, cos, sin):
    x1, x2 = x[..., :ROPE_FREQS], x[..., ROPE_FREQS:]
    c, sn = cos[:, None, :], sin[:, None, :]
    return jnp.concatenate([x1 * c - x2 * sn, x2 * c + x1 * sn], axis=-1)


def apply_axial_rope(x, tabs):
    cos_r, sin_r, cos_c, sin_c = tabs
    half = HEAD_DIM // 2
    return jnp.concatenate([rope_half(x[..., :half], cos_r, sin_r),
                            rope_half(x[..., half:], cos_c, sin_c)], axis=-1)


def gqa_attention(q, k, v, g_q, g_k, tabs):
    b, s, _ = q.shape
    q = rmsnorm(q.reshape(b, s, N_Q_HEADS, HEAD_DIM), g_q)
    k = rmsnorm(k.reshape(b, s, N_KV_HEADS, HEAD_DIM), g_k)
    v = v.reshape(b, s, N_KV_HEADS, HEAD_DIM)
    q = apply_axial_rope(q, tabs)
    k = apply_axial_rope(k, tabs)
    qb = q.reshape(b, s // Q_BLOCK, Q_BLOCK, N_KV_HEADS, Q_PER_KV, HEAD_DIM).transpose(1, 0, 2, 3, 4, 5)
    scale = HEAD_DIM ** -0.5

    def block(qblk):
        sc = jnp.einsum('bqkgd,bskd->bkgqs', qblk, k).astype(jnp.float32) * scale
        p = jax.nn.softmax(sc, axis=-1).astype(v.dtype)
        return jnp.einsum('bkgqs,bskd->bqkgd', p, v)

    o = lax.map(block, qb)
    return o.transpose(1, 0, 2, 3, 4, 5).reshape(b, s, Q_WIDTH)


def hybrid_mixer(h, w_in, sgu_w, sgu_b, sgu_g, g_q, g_k, g_mix, w_out, tabs):
    z = h @ w_in
    o0 = FNET_WIDTH
    o1 = o0 + 2 * SGU_WIDTH
    o2 = o1 + Q_WIDTH
    o3 = o2 + KV_WIDTH
    y_f = fourier_mix(z[..., :o0])
    y_s = spatial_gating(z[..., o0:o1], sgu_w, sgu_b, sgu_g)
    y_a = gqa_attention(z[..., o1:o2], z[..., o2:o3], z[..., o3:], g_q, g_k, tabs)
    a0 = FNET_WIDTH
    a1 = a0 + SGU_WIDTH
    y = jnp.concatenate([rmsnorm(y_f, g_mix[:a0]),
                         rmsnorm(y_s, g_mix[a0:a1]),
                         rmsnorm(y_a, g_mix[a1:])], axis=-1)
    return y @ w_out


def swiglu(h, w_gate, w_up, w_down):
    return (jax.nn.silu(h @ w_gate) * (h @ w_up)) @ w_down


def setup_inputs(seed: int = 0) -> dict:
    key = jax.random.key(seed)
    ks = jax.random.split(key, 17)
    f32 = jnp.float32

    def gain(k, shape):
        return (1.0 + 0.02 * jax.random.normal(k, shape, f32)).astype(f32)

    def dense(k, shape, fan_in):
        return (jax.random.normal(k, shape, f32) * fan_in ** -0.5).astype(f32)

    return {
        "x": jax.random.normal(ks[0], (BATCH, SEQ, D_MODEL), f32),
        "g_pre_mix": gain(ks[1], (DEPTH, D_MODEL)),
        "w_in": dense(ks[2], (DEPTH, D_MODEL, D_IN), D_MODEL),
        "sgu_w": dense(ks[3], (DEPTH, SGU_HEADS, SGU_CHUNK, SGU_CHUNK), SGU_CHUNK),
        "sgu_b": gain(ks[4], (DEPTH, SGU_HEADS, SGU_CHUNK)),
        "sgu_g": gain(ks[5], (DEPTH, SGU_WIDTH)),
        "g_q": gain(ks[6], (DEPTH, HEAD_DIM)),
        "g_k": gain(ks[7], (DEPTH, HEAD_DIM)),
        "g_mix": gain(ks[8], (DEPTH, D_MIX)),
        "w_out": dense(ks[9], (DEPTH, D_MIX, D_MODEL), D_MIX),
        "g_post_mix": gain(ks[10], (DEPTH, D_MODEL)),
        "g_pre_ffn": gain(ks[11], (DEPTH, D_MODEL)),
        "w_gate": dense(ks[12], (DEPTH, D_MODEL, D_FF), D_MODEL),
        "w_up": dense(ks[13], (DEPTH, D_MODEL, D_FF), D_MODEL),
        "w_down": dense(ks[14], (DEPTH, D_FF, D_MODEL), D_FF),
        "g_post_ffn": gain(ks[15], (DEPTH, D_MODEL)),
    }


def reference(x, g_pre_mix, w_in, sgu_w, sgu_b, sgu_g, g_q, g_k, g_mix, w_out,
              g_post_mix, g_pre_ffn, w_gate, w_up, w_down, g_post_ffn):
    tabs = axial_rope_tables(x.shape[1], x.dtype)
    for l in range(DEPTH):
        h = rmsnorm(x, g_pre_mix[l])
        m = hybrid_mixer(h, w_in[l], sgu_w[l], sgu_b[l], sgu_g[l], g_q[l], g_k[l],
                         g_mix[l], w_out[l], tabs)
        x = x + rmsnorm(m, g_post_mix[l])
        h = rmsnorm(x, g_pre_ffn[l])
        f = swiglu(h, w_gate[l], w_up[l], w_down[l])
        x = x + rmsnorm(f, g_post_ffn[l])
    return x
```

```python
import contextlib
import numpy as np
import ml_dtypes
import concourse.bass as bass
import concourse.mybir as mybir
from concourse.bass_utils import run_bass_kernel_spmd

F32 = mybir.dt.float32
BF16 = mybir.dt.bfloat16
ALU = mybir.AluOpType
AF = mybir.ActivationFunctionType
AX = mybir.AxisListType

S = 4096
D = 1024
NT = 32
DIN = 1536
DFF = 2816
NFF = 22
EPS = 1e-6
DEPTH = 2

import os
SAME_ENG_SYNC = os.environ.get('SES', '1') == '1'


class Res:
    __slots__ = ("name", "w", "rs", "excl")

    def __init__(self, name="", excl=False):
        self.name = name
        self.w = None
        self.rs = []
        self.excl = excl


def RP():
    return Res("psum", True)


class Op:
    __slots__ = ("eng", "fn", "deps", "signal", "val", "sem", "is_dma", "name")


class Chan:
    def __init__(self, sem, name):
        self.sem = sem
        self.count = 0
        self.last = None
        self.name = name
        self.eng = None
        self.fresh = False


ENGS = ("pe", "act", "dve", "pool", "sp")


class Prog:
    def __init__(self, nc, stack):
        self.nc = nc
        self.stack = stack
        self.sems = {e: stack.enter_context(nc.semaphore("prog_" + e)) for e in ENGS}
        self.counts = {e: 0 for e in ENGS}
        self.free_chans = []
        self.nchan = 0
        self.ops = None
        self.phase_chans = None
        self.scopes = False

    def chan(self, name="c", fresh=False):
        if fresh:
            sem = self.stack.enter_context(self.nc.semaphore("fch%d" % self.nchan))
            self.nchan += 1
            c = Chan(sem, name)
            c.fresh = True
            self.phase_chans.append(c)
            return c
        if self.free_chans:
            c = self.free_chans.pop()
            c.name = name
            c.last = None
            c.eng = None
        else:
            sem = self.stack.enter_context(self.nc.semaphore("ch%d" % self.nchan))
            self.nchan += 1
            c = Chan(sem, name)
        self.phase_chans.append(c)
        return c

    def chans(self, n, name="c", fresh=False):
        return [self.chan("%s%d" % (name, i), fresh) for i in range(n)]

    def begin(self):
        self.ops = {e: [] for e in ENGS}
        self.phase_chans = []

    def _mk(self, eng, fn, reads, writes, is_dma, name):
        o = Op()
        o.eng = eng
        o.fn = fn
        o.signal = is_dma
        o.val = None
        o.sem = None
        o.is_dma = is_dma
        o.name = name
        deps = []
        seen = set()

        def add(d):
            if d is None or id(d) in seen:
                return
            seen.add(id(d))
            if d.eng == eng and not d.is_dma and not is_dma:
                if eng == "pe" or not SAME_ENG_SYNC:
                    return
            deps.append(d)

        for r in reads:
            add(r.w)
            if r.excl:
                for x in r.rs:
                    if x.eng != eng:
                        add(x)
        for w in writes:
            add(w.w)
            for x in w.rs:
                add(x)
        for r in reads:
            if not is_dma:
                r.rs = [x for x in r.rs if not (x.eng == eng and not x.is_dma)]
            r.rs.append(o)
        for w in writes:
            w.w = o
            w.rs = []
        o.deps = deps
        for d in deps:
            d.signal = True
        self.ops[eng].append(o)
        return o

    def op(self, eng, fn, reads=(), writes=(), name=""):
        return self._mk(eng, fn, reads, writes, False, name)

    def dma(self, eng, out, in_, reads=(), writes=(), chan=None, name="", **kw):
        assert chan is not None
        if eng == "pool":
            assert chan.fresh and chan.count == 0, "pool DMA needs a fresh chan"

        def fn(e):
            return e.dma_start(out=out, in_=in_, **kw)

        o = self._mk(eng, fn, reads, writes, True, name)
        if chan.last is not None:
            o.deps.append(chan.last)
        assert chan.eng in (None, eng)
        chan.eng = eng
        chan.count += 1
        chan.last = o
        o.sem = chan.sem
        o.val = 16 * chan.count
        return o

    def finalize(self):
        for e in ENGS:
            cnt = self.counts[e]
            for o in self.ops[e]:
                if o.is_dma:
                    continue
                if o.signal:
                    cnt += 1
                    o.val = cnt
                    o.sem = self.sems[e]
            self.counts[e] = cnt

    def emit(self, engname, e):
        waited = {}
        for o in self.ops[engname]:
            need = {}
            for d in o.deps:
                assert d.val is not None, (d.name, o.name)
                if need.get(d.sem, 0) < d.val:
                    need[d.sem] = d.val
            for s, v in need.items():
                if waited.get(s, 0) >= v:
                    continue
                e.wait_ge(s, v)
                waited[s] = v
            ins = o.fn(e)
            if o.signal:
                ins.then_inc(o.sem, 16 if o.is_dma else 1)
        for c in self.phase_chans:
            if c.eng == engname and c.count > 0:
                e.wait_ge(c.sem, 16 * c.count)

    def run_block(self, name=None):
        self.finalize()
        nc = self.nc
        self.nblk = getattr(self, "nblk", 0) + 1
        scope = nc.named_scope("%02d_%s" % (self.nblk, name or "blk")) if self.scopes else contextlib.nullcontext()
        with scope, nc.Block() as block:
            @block.tensor
            def _(e):
                self.emit("pe", e)

            @block.scalar
            def _(e):
                self.emit("act", e)

            @block.vector
            def _(e):
                self.emit("dve", e)

            @block.gpsimd
            def _(e):
                self.emit("pool", e)

            @block.sync
            def _(e):
                self.emit("sp", e)
        self.free_chans.extend(c for c in self.phase_chans if not c.fresh)
        self.phase_chans = None
        self.ops = None


def delayed(n, gen):
    for _ in range(n):
        yield
    yield from gen


def bcast(ap, axis, n):
    lst = [list(x) for x in ap.ap]
    lst.insert(axis, [0, n])
    return bass.AP(ap.tensor, ap.offset, lst)


def pbcast(ap, n=128):
    lst = [[0, n]] + [list(x) for x in ap.ap]
    return bass.AP(ap.tensor, ap.offset, lst)


def _bf(a):
    return np.ascontiguousarray(a.astype(np.float32)).astype(ml_dtypes.bfloat16)


def make_consts():
    c = {}
    c["ident"] = _bf(np.eye(128))
    c["identf"] = np.eye(128, dtype=np.float32)
    tok = np.arange(S)
    row = (tok // 64).astype(np.float32)
    col = (tok % 64).astype(np.float32)
    freqs = (np.float32(10000.0) ** (-np.arange(16, dtype=np.float32) / np.float32(16))).astype(np.float32)
    ar = (row[:, None] * freqs).astype(np.float32)
    ac = (col[:, None] * freqs).astype(np.float32)
    cr, sr, cc, sc = np.cos(ar), np.sin(ar), np.cos(ac), np.sin(ac)
    COS = np.concatenate([cr, cr, cc, cc], 1)
    SINS = np.concatenate([-sr, sr, -sc, sc], 1)
    c["cos"] = np.ascontiguousarray(COS.reshape(NT, 128, 64).transpose(1, 0, 2)).astype(np.float32)
    c["sins"] = np.ascontiguousarray(SINS.reshape(NT, 128, 64).transpose(1, 0, 2)).astype(np.float32)
    t = np.arange(32)
    ang = 2 * np.pi * np.outer(t, t) / 32.0
    c["f1"] = _bf(np.concatenate([np.cos(ang), -np.sin(ang)], 1))
    p = np.arange(128)[:, None, None]
    kt = np.arange(32)[None, :, None]
    kp = np.arange(128)[None, None, :]
    th = 2 * np.pi * ((p * (kt + 32 * kp)) % 4096) / 4096.0
    c["f2"] = _bf(np.stack([np.sin(th), np.cos(th), -np.sin(th)], 2))
    cidx = np.arange(64)
    a64 = 2 * np.pi * np.outer(cidx, cidx) / 64.0
    C64 = np.cos(a64) / 512.0
    S64 = np.sin(a64) / 512.0
    Cb = np.zeros((128, 128)); Sb = np.zeros((128, 128))
    for g in range(2):
        Cb[64 * g:64 * g + 64, 64 * g:64 * g + 64] = C64
        Sb[64 * g:64 * g + 64, 64 * g:64 * g + 64] = S64
    c["f3"] = _bf(np.stack([Cb, Sb], 1))
    return c


CONST_SHAPES = {
    "ident": ([128, 128], BF16), "identf": ([128, 128], F32),
    "cos": ([128, NT, 64], F32), "sins": ([128, NT, 64], F32),
    "f1": ([32, 64], BF16), "f2": ([128, 32, 3, 128], BF16), "f3": ([128, 2, 128], BF16),
}

PARAM_SHAPES = {
    "g_pre_mix": [DEPTH, D], "w_in": [DEPTH, D, DIN], "sgu_w": [DEPTH, 4, 128, 128],
    "sgu_b": [DEPTH, 4, 128], "sgu_g": [DEPTH, 256], "g_q": [DEPTH, 64], "g_k": [DEPTH, 64],
    "g_mix": [DEPTH, D], "w_out": [DEPTH, D, D], "g_post_mix": [DEPTH, D],
    "g_pre_ffn": [DEPTH, D], "w_gate": [DEPTH, D, DFF], "w_up": [DEPTH, D, DFF],
    "w_down": [DEPTH, DFF, D], "g_post_ffn": [DEPTH, D],
}


def build(n_layers=DEPTH, stop_after=None, debug=False):
    nc = bass.Bass("TRN2", target_bir_lowering=False)
    T = {}
    T["x"] = nc.dram_tensor("x", [S, D], F32, kind="ExternalInput").ap()
    for k, shp in PARAM_SHAPES.items():
        T[k] = nc.dram_tensor(k, shp, F32, kind="ExternalInput").ap()
    for k, (shp, dt) in CONST_SHAPES.items():
        T[k] = nc.dram_tensor(k, shp, dt, kind="ExternalInput").ap()
    out = nc.dram_tensor("out", [S, D], F32, kind="ExternalOutput").ap()
    kind_dbg = "ExternalOutput" if debug else "Internal"
    xf_d = nc.dram_tensor("xf_d", [NT, 128, 256], BF16, kind=kind_dbg).ap()
    t1_d = nc.dram_tensor("t1_d", [64, 128, 256], BF16, kind=kind_dbg).ap()
    y_d = nc.dram_tensor("y_d", [S, D], BF16, kind=kind_dbg).ap()
    dbg = {}
    if debug:
        dbg["qT"] = nc.dram_tensor("dbg_qT", [128, 4, S], BF16, kind="ExternalOutput").ap()
        dbg["kT"] = nc.dram_tensor("dbg_kT", [128, S], BF16, kind="ExternalOutput").ap()
        dbg["V"] = nc.dram_tensor("dbg_V", [128, NT, 2, 128], BF16, kind="ExternalOutput").ap()

    with contextlib.ExitStack() as stack:
        P = Prog(nc, stack)
        P.scopes = False

        uid = [0]

        def sb(name, shape, dt, st=None, side=None):
            uid[0] += 1
            return (st or stack).enter_context(nc.sbuf_tensor("s%d_%s" % (uid[0], name), shape, dt, side=side))

        def ps(name, shape, dt, st):
            uid[0] += 1
            return st.enter_context(nc.psum_tensor("p%d_%s" % (uid[0], name), shape, dt))

        ident = sb("ident", [128, 128], BF16)
        identf = sb("identf", [128, 128], F32)
        epsb = sb("epsb", [128, 1], F32)

        def layer(l, src):
            with contextlib.ExitStack() as wst:
                W = {}
                with contextlib.ExitStack() as lst:
                    qT = sb("qT", [128, 4, S], BF16, lst)
                    kT = sb("kT", [128, S], BF16, lst)
                    V = sb("V", [128, NT, 2, 128], BF16, lst)
                    phase_A(l, src, qT, kT, V)
                    if stop_after == "A":
                        if debug:
                            dump(qT, kT, V)
                        return False
                    phase_F(l)
                    if stop_after == "F":
                        return False
                    W["g"] = sb("w_g", [128, 8, DFF], BF16, wst, side="right")
                    W["u"] = sb("w_u", [128, 8, DFF], BF16, wst, side="right")
                    W["o"] = sb("w_o", [128, 8, D], BF16, wst, side="right")
                    phase_B(l, qT, kT, V, W)
                    if stop_after == "B":
                        return False
                W["d"] = sb("w_d", [128, NFF, D], BF16, wst, side="right")
                phase_C(l, src, W)
                if stop_after == "C":
                    return False
                phase_D(l, W)
            return True

        def dump(qT, kT, V):
            P.begin()
            c = P.chans(3, "dump")
            P.dma("sp", dbg["qT"], qT[:], chan=c[0])
            P.dma("sp", dbg["kT"], kT[:], chan=c[1])
            P.dma("sp", dbg["V"], V[:], chan=c[2])
            P.run_block()

        def phase_setup():
            P.begin()
            c = P.chans(2, "setup")
            r1, r2 = Res(), Res()
            P.dma("sp", ident[:], T["ident"], writes=[r1], chan=c[0])
            P.dma("sp", identf[:], T["identf"], writes=[r2], chan=c[1])
            P.op("dve", lambda e: e.memset(epsb[:], EPS))
            P.run_block()

        def phase_A(l, src, qT, kT, V):
            with contextlib.ExitStack() as st:
                w_in = sb("w_in", [128, 8, DIN], BF16, st)
                cos = sb("cos", [128, NT, 64], F32, st)
                sins = sb("sins", [128, NT, 64], F32, st)
                gpp = sb("gpp", [128, 8], F32, st)
                gv = sb("gv", [128, 256], F32, st)
                gms = sb("gms", [128, 256], F32, st)
                gtab = sb("gtab", [128, 10, 64], F32, st)
                ws_f = sb("ws_f", [128, 4, 128], F32, st)
                ws_b = sb("ws_b", [128, 4, 128], BF16, st)
                wsT = sb("wsT", [128, 4, 128], BF16, st)
                bpp = sb("bpp", [128, 4], F32, st)
                junk = sb("junk", [128, D], BF16, st)
                junk2 = sb("junk2", [128, 256], BF16, st)
                two = lambda name, shape, dt: [sb("%s%d" % (name, i), shape, dt, st) for i in range(2)]
                xs = two("xs", [128, D], F32)
                ss = two("ss", [128, 1], F32)
                rstd = two("rstd", [128, 1], F32)
                h = two("h", [128, D], BF16)
                hT = two("hT", [128, 8, 128], BF16)
                zf = two("zf", [128, 256], BF16)
                ug = two("ug", [128, 256], F32)
                vg = two("vg", [128, 256], F32)
                bst = two("bst", [128, 6], F32)
                mv = two("mv", [128, 2], F32)
                rsv = two("rsv", [128, 1], F32)
                vn = two("vn", [128, 256], F32)
                vnb = two("vnb", [128, 256], BF16)
                ys = two("ys", [128, 256], F32)
                ss2 = two("ss2", [128, 1], F32)
                rs2 = two("rs2", [128, 1], F32)
                ysb = two("ysb", [128, 256], BF16)
                qk = two("qk", [128, 640], F32)
                qa = two("qa", [128, 640], F32)
                qb = two("qb", [128, 640], F32)
                qss = two("qss", [128, 10], F32)
                qrs = two("qrs", [128, 10], F32)
                qkr = two("qkr", [128, 640], BF16)
                pT = [ps("pT%d" % i, [128, 8, 128], BF16, st) for i in range(2)]
                pz = [ps("pz%d" % i, [128, 512], F32, st) for i in range(3)]
                psv = [ps("psv%d" % i, [128, 512], F32, st) for i in range(2)]
                pT2 = ps("pT2", [128, 8, 128], BF16, st)

                P.begin()
                R = lambda n="": Res(n)
                R2 = lambda: [Res(), Res()]
                r_w = [R("w%d" % i) for i in range(8)]
                cw = P.chans(3, "w_in", fresh=True)
                r_wc = [R("wc%d" % i) for i in range(3)]
                for c3 in range(3):
                    P.dma("pool", w_in[:, :, c3 * 512:(c3 + 1) * 512],
                          T["w_in"][l, :, c3 * 512:(c3 + 1) * 512].rearrange("(k p) n -> p k n", p=128),
                          writes=[r_wc[c3]], chan=cw[c3])
                cc = P.chans(12, "cst")
                r_cos, r_sins, r_gpre, r_gv, r_gms, r_gtab, r_wsf, r_bpp = [R() for _ in range(8)]
                P.dma("sp", gpp[:], T["g_pre_mix"][l].rearrange("(k p) -> p k", p=128), writes=[r_gpre], chan=cc[2],
                      allow_slow_non_contiguous=True)
                r_w2 = [R("w2_%d" % i) for i in range(3)]
                for c3 in range(3):
                    for kc in range(8):
                        P.op("dve", lambda e, kc=kc, c3=c3: e.tensor_scalar(
                            out=w_in[:, kc, c3 * 512:(c3 + 1) * 512], in0=w_in[:, kc, c3 * 512:(c3 + 1) * 512],
                            scalar1=gpp[:, kc:kc + 1], scalar2=None, op0=ALU.mult),
                            reads=[r_wc[c3], r_gpre], writes=[r_w2[c3]])
                P.dma("sp", ws_f[:], T["sgu_w"][l].rearrange("h p q -> p h q"), writes=[r_wsf], chan=cc[6])
                P.dma("sp", cos[:], T["cos"], writes=[r_cos], chan=cc[0])
                P.dma("sp", sins[:], T["sins"], writes=[r_sins], chan=cc[1])
                P.dma("sp", gv[:], pbcast(T["sgu_g"][l]), writes=[r_gv], chan=cc[3])
                P.dma("sp", gms[:], pbcast(T["g_mix"][l, 256:512]), writes=[r_gms], chan=cc[4])
                for hh in range(10):
                    srcg = T["g_q"][l] if hh < 8 else T["g_k"][l]
                    P.dma("sp", gtab[:, hh, :], pbcast(srcg), writes=[r_gtab], chan=cc[5])
                P.dma("sp", bpp[:], T["sgu_b"][l].rearrange("h p -> p h"), writes=[r_bpp], chan=cc[7],
                      allow_slow_non_contiguous=True)
                r_wsb, r_wsT = R(), R()
                r_pT = [RP(), RP()]
                r_psv = [RP(), RP()]
                P.op("dve", lambda e: e.tensor_copy(out=ws_b[:], in_=ws_f[:]), reads=[r_wsf], writes=[r_wsb])
                for hh in range(4):
                    P.op("pe", lambda e, hh=hh: e.transpose(out=pT[0][:, hh, :], in_=ws_b[:, hh, :], identity=ident[:]),
                         reads=[r_wsb], writes=[r_pT[0]])
                P.op("dve", lambda e: e.tensor_copy(out=wsT[:], in_=pT[0][:, 0:4, :]), reads=[r_pT[0]], writes=[r_wsT])
                r_V = R("V")
                P.op("pool", lambda e: e.memset(V[:, :, :, 64:128], 1.0), writes=[r_V])

                r_xs, r_ss, r_rstd, r_h, r_hT, r_zf, r_ysb, r_qkr = [R2() for _ in range(8)]
                r_ug, r_vg, r_bst, r_mv, r_rsv, r_vn, r_vnb, r_ys, r_ss2, r_rs2 = [R2() for _ in range(10)]
                r_qk, r_qa, r_qb, r_qss, r_qrs = [R2() for _ in range(5)]
                r_junk, r_junk2, r_pT2 = R(), R(), RP()
                r_hf = R2()
                r_pz = [RP() for _ in range(3)]
                r_qT, r_kT = R(), R()
                cx = P.chans(2, "xs")
                czf = P.chans(2, "zf")
                cys = P.chans(2, "ys")
                qk3 = lambda a: a[:].rearrange("p (h d) -> p h d", h=10)
                qk5 = lambda a: a[:].rearrange("p (h f x d) -> p (h f) x d", h=10, f=2, x=2)
                sin4 = lambda t: sins[:, t, :].rearrange("p (f x d) -> p f x d", f=2, x=2)

                def ch_load(t):
                    s = t % 2
                    P.dma("sp", xs[s][:], src[t * 128:(t + 1) * 128, :], writes=[r_xs[s]], chan=cx[s])
                    yield

                def ch_front(t):
                    s = t % 2
                    P.op("act", lambda e: e.activation(out=junk[:], in_=xs[s][:], func=AF.Square, accum_out=ss[s][:]),
                         reads=[r_xs[s]], writes=[r_junk, r_ss[s]])
                    yield
                    P.op("act", lambda e: e.activation(out=rstd[s][:], in_=ss[s][:], func=AF.Ln, scale=1.0 / D,
                                                       bias=epsb[:, 0:1]), reads=[r_ss[s]], writes=[r_rstd[s]])
                    yield
                    P.op("act", lambda e: e.activation(out=rstd[s][:], in_=rstd[s][:], func=AF.Exp, scale=-0.5),
                         reads=[r_rstd[s]], writes=[r_rstd[s]])
                    yield
                    P.op("act", lambda e: e.activation(out=h[s][:], in_=xs[s][:], func=AF.Copy, scale=rstd[s][:, 0:1]),
                         reads=[r_xs[s], r_rstd[s]], writes=[r_h[s]])
                    yield

                def ch_front2(t):
                    s = t % 2
                    for j in range(8):
                        P.op("pe", lambda e, j=j: e.transpose(out=pT[s][:, j, :], in_=h[s][:, j * 128:(j + 1) * 128],
                                                              identity=ident[:]), reads=[r_h[s]], writes=[r_pT[s]])
                    yield
                    P.op("act", lambda e: e.copy(out=hT[s][:], in_=pT[s][:]), reads=[r_pT[s]], writes=[r_hT[s]])
                    yield

                def ch_front2b(t):
                    s = t % 2
                    for c3 in range(3):
                        for kc in range(8):
                            P.op("pe", lambda e, c3=c3, kc=kc: e.matmul(
                                out=pz[c3][:], lhsT=hT[s][:, kc, :], rhs=w_in[:, kc, c3 * 512:(c3 + 1) * 512],
                                start=(kc == 0), stop=(kc == 7)), reads=[r_hT[s], r_w2[c3]], writes=[r_pz[c3]])
                        yield

                def ch_gelu(t):
                    s = t % 2
                    P.op("act", lambda e: e.activation(out=ug[s][:], in_=pz[0][:, 256:512], func=AF.Gelu_apprx_tanh),
                         reads=[r_pz[0]], writes=[r_ug[s]])
                    P.op("act", lambda e: e.activation(out=vg[s][:], in_=pz[1][:, 0:256], func=AF.Gelu_apprx_tanh),
                         reads=[r_pz[1]], writes=[r_vg[s]])
                    yield

                def ch_evac(t):
                    s = t % 2
                    P.op("act", lambda e: e.copy(out=zf[s][:], in_=pz[0][:, 0:256]), reads=[r_pz[0]], writes=[r_zf[s]])
                    P.op("act", lambda e: e.copy(out=qk[s][:, 0:256], in_=pz[1][:, 256:512]), reads=[r_pz[1]], writes=[r_qk[s]])
                    P.op("act", lambda e: e.copy(out=qk[s][:, 256:640], in_=pz[2][:, 0:384]), reads=[r_pz[2]], writes=[r_qk[s]])
                    P.op("act", lambda e: e.copy(out=V[:, t, :, 0:64],
                                                 in_=pz[2][:, 384:512].rearrange("p (k d) -> p k d", k=2)),
                         reads=[r_pz[2]], writes=[r_V])
                    yield

                def ch_misc(t):
                    s = t % 2
                    P.dma("sp", xf_d[t], zf[s][:], reads=[r_zf[s]], chan=czf[s])
                    yield

                def ch_sgu(t):
                    s = t % 2
                    P.op("dve", lambda e: e.bn_stats(out=bst[s][:], in_=vg[s][:]), reads=[r_vg[s]], writes=[r_bst[s]])
                    yield
                    P.op("dve", lambda e: e.bn_aggr(out=mv[s][:], in_=bst[s][:]), reads=[r_bst[s]], writes=[r_mv[s]])
                    yield
                    P.op("act", lambda e: e.activation(out=rsv[s][:], in_=mv[s][:, 1:2], func=AF.Ln, bias=epsb[:, 0:1]),
                         reads=[r_mv[s]], writes=[r_rsv[s]])
                    yield
                    P.op("act", lambda e: e.activation(out=rsv[s][:], in_=rsv[s][:], func=AF.Exp, scale=-0.5),
                         reads=[r_rsv[s]], writes=[r_rsv[s]])
                    yield
                    P.op("dve", lambda e: e.tensor_scalar(out=vn[s][:], in0=vg[s][:], scalar1=mv[s][:, 0:1], scalar2=rsv[s][:, 0:1],
                                                          op0=ALU.subtract, op1=ALU.mult),
                         reads=[r_vg[s], r_mv[s], r_rsv[s]], writes=[r_vn[s]])
                    yield
                    P.op("dve", lambda e: e.tensor_tensor(out=vnb[s][:], in0=vn[s][:], in1=gv[:], op=ALU.mult),
                         reads=[r_vn[s], r_gv], writes=[r_vnb[s]])
                    yield
                    for hh in range(4):
                        P.op("pe", lambda e, hh=hh: e.matmul(out=psv[s][:, hh * 64:(hh + 1) * 64], lhsT=wsT[:, hh, :],
                                                             rhs=vnb[s][:, hh * 64:(hh + 1) * 64], start=True, stop=True),
                             reads=[r_wsT, r_vnb[s]], writes=[r_psv[s]])
                    yield
                    P.op("dve", lambda e: e.tensor_tensor(out=ys[s][:].rearrange("p (h c) -> p h c", h=4),
                                                          in0=psv[s][:, 0:256].rearrange("p (h c) -> p h c", h=4),
                                                          in1=bcast(bpp[:], 2, 64), op=ALU.add),
                         reads=[r_psv[s], r_bpp], writes=[r_ys[s]])
                    yield
                    P.op("dve", lambda e: e.tensor_tensor(out=ys[s][:], in0=ys[s][:], in1=ug[s][:], op=ALU.mult),
                         reads=[r_ys[s], r_ug[s]], writes=[r_ys[s]])
                    yield

                def ch_sgu_b(t):
                    s = t % 2
                    P.op("act", lambda e: e.activation(out=junk2[:], in_=ys[s][:], func=AF.Square, accum_out=ss2[s][:]),
                         reads=[r_ys[s]], writes=[r_junk2, r_ss2[s]])
                    yield
                    P.op("act", lambda e: e.activation(out=rs2[s][:], in_=ss2[s][:], func=AF.Ln, scale=1.0 / 256, bias=epsb[:, 0:1]),
                         reads=[r_ss2[s]], writes=[r_rs2[s]])
                    yield
                    P.op("act", lambda e: e.activation(out=rs2[s][:], in_=rs2[s][:], func=AF.Exp, scale=-0.5),
                         reads=[r_rs2[s]], writes=[r_rs2[s]])
                    yield
                    P.op("dve", lambda e: e.scalar_tensor_tensor(out=ysb[s][:], in0=ys[s][:], scalar=rs2[s][:, 0:1],
                                                                 in1=gms[:], op0=ALU.mult, op1=ALU.mult),
                         reads=[r_ys[s], r_rs2[s], r_gms], writes=[r_ysb[s]])
                    P.dma("sp", y_d[t * 128:(t + 1) * 128, 256:512], ysb[s][:], reads=[r_ysb[s]], chan=cys[s])
                    yield

                def ch_qk(t):
                    s = t % 2
                    P.op("dve", lambda e: e.tensor_tensor(out=qa[s][:], in0=qk[s][:], in1=qk[s][:], op=ALU.mult),
                         reads=[r_qk[s]], writes=[r_qa[s]])
                    yield
                    P.op("dve", lambda e: e.tensor_reduce(out=qss[s][:], in_=qk3(qa[s]), axis=AX.X, op=ALU.add),
                         reads=[r_qa[s]], writes=[r_qss[s]])
                    yield
                    P.op("act", lambda e: e.activation(out=qrs[s][:], in_=qss[s][:], func=AF.Ln, scale=1.0 / 64, bias=epsb[:, 0:1]),
                         reads=[r_qss[s]], writes=[r_qrs[s]])
                    yield
                    P.op("act", lambda e: e.activation(out=qrs[s][:], in_=qrs[s][:], func=AF.Exp, scale=-0.5),
                         reads=[r_qrs[s]], writes=[r_qrs[s]])
                    yield
                    P.op("dve", lambda e: e.tensor_tensor(out=qk3(qa[s]), in0=qk3(qk[s]), in1=bcast(qrs[s][:], 2, 64), op=ALU.mult),
                         reads=[r_qk[s], r_qrs[s]], writes=[r_qa[s]])
                    yield
                    P.op("dve", lambda e: e.tensor_tensor(out=qa[s][:], in0=qa[s][:], in1=gtab[:].rearrange("p h d -> p (h d)"),
                                                           op=ALU.mult), reads=[r_qa[s], r_gtab], writes=[r_qa[s]])
                    yield
                    P.op("dve", lambda e: e.tensor_tensor(out=qk3(qb[s]), in0=qk3(qa[s]), in1=bcast(cos[:, t, :], 1, 10),
                                                          op=ALU.mult), reads=[r_qa[s], r_cos], writes=[r_qb[s]])
                    yield

                    def sw(e, xo, xi):
                        o_ = qk5(qk[s])[:, :, xo, :].rearrange("p (h f) d -> p h f d", h=10)
                        i_ = qk5(qa[s])[:, :, xi, :].rearrange("p (h f) d -> p h f d", h=10)
                        s_ = bcast(sin4(t)[:, :, xo, :], 1, 10)
                        return e.tensor_tensor(out=o_, in0=i_, in1=s_, op=ALU.mult)
                    P.op("dve", lambda e: sw(e, 0, 1), reads=[r_qa[s], r_sins, r_qk[s]], writes=[r_qk[s]])
                    yield
                    P.op("dve", lambda e: sw(e, 1, 0), reads=[r_qa[s], r_sins, r_qk[s]], writes=[r_qk[s]])
                    yield
                    P.op("dve", lambda e: e.tensor_tensor(
                        out=qkr[s][:, 0:512].rearrange("p (j e d) -> p j e d", j=4, e=2),
                        in0=qb[s][:, 0:512].rearrange("p (e j d) -> p j e d", e=2, j=4),
                        in1=qk[s][:, 0:512].rearrange("p (e j d) -> p j e d", e=2, j=4), op=ALU.add),
                        reads=[r_qb[s], r_qk[s]], writes=[r_qkr[s]])
                    yield
                    P.op("dve", lambda e: e.tensor_tensor(out=qkr[s][:, 512:640], in0=qb[s][:, 512:640],
                                                          in1=qk[s][:, 512:640], op=ALU.add),
                         reads=[r_qb[s], r_qk[s], r_qkr[s]], writes=[r_qkr[s]])
                    yield

                def ch_tail(t):
                    s = t % 2
                    for j in range(5):
                        P.op("pe", lambda e, j=j: e.transpose(out=pT2[:, j, :], in_=qkr[s][:, j * 128:(j + 1) * 128],
                                                              identity=ident[:]), reads=[r_qkr[s]], writes=[r_pT2])
                    yield
                    P.op("act", lambda e: e.copy(out=qT[:, :, t * 128:(t + 1) * 128], in_=pT2[:, 0:4, :]),
                         reads=[r_pT2], writes=[r_qT])
                    yield
                    P.op("dve", lambda e: e.tensor_copy(out=kT[:, t * 128:(t + 1) * 128], in_=pT2[:, 4, :]),
                         reads=[r_pT2], writes=[r_kT])
                    yield

                def rr(gens):
                    gens = list(gens)
                    while gens:
                        for g in list(gens):
                            try:
                                next(g)
                            except StopIteration:
                                gens.remove(g)

                rr([ch_load(0)])
                rr([ch_load(1), ch_front(0)])
                rr([ch_load(2), ch_front(1), ch_front2(0)])
                rr([ch_front2b(0)])
                for t in range(NT + 1):
                    if t == NT:
                        rr([ch_tail(t - 1), ch_sgu_b(t - 1)])
                        break
                    if t + 1 < NT:
                        rr([ch_front2(t + 1)])
                    rr([ch_gelu(t)])
                    rr([ch_evac(t)])
                    gens = [ch_misc(t), ch_sgu(t), ch_qk(t)]
                    if t >= 1:
                        gens.append(delayed(2, ch_tail(t - 1)))
                        gens.append(delayed(2, ch_sgu_b(t - 1)))
                    if t + 1 < NT:
                        gens.insert(0, ch_front2b(t + 1))
                    if t + 2 < NT:
                        gens.insert(0, ch_front(t + 2))
                    if t + 3 < NT:
                        gens.insert(0, ch_load(t + 3))
                    rr(gens)
                P.run_block("A")

        def phase_F(l):
            with contextlib.ExitStack() as st:
                f1 = sb("f1", [32, 64], BF16, st)
                f2 = sb("f2", [128, 32, 3, 128], BF16, st)
                f3 = sb("f3", [128, 2, 128], BF16, st)
                xin = [sb("xin%d" % i, [32, 4096], BF16, st) for i in range(3)]
                t1s = [sb("t1s%d" % i, [64, 4096], BF16, st) for i in range(2)]
                T2 = sb("T2", [128, 64, 256], BF16, st)
                GT = sb("GT", [128, 2, 2, S], BF16, st)
                gmf = sb("gmf", [128, 256], F32, st)
                junk = sb("junkF", [128, 256], BF16, st)
                ssF = [sb("ssF%d" % i, [128, 1], F32, st) for i in range(2)]
                rsF = [sb("rsF%d" % i, [128, 1], F32, st) for i in range(2)]
                yfb = [sb("yfb%d" % i, [128, 256], BF16, st) for i in range(2)]
                p1 = [ps("p1_%d" % i, [64, 512], F32, st) for i in range(2)]
                p2 = [ps("p2_%d" % i, [128, 512], F32, st) for i in range(2)]
                p3 = [ps("p3_%d" % i, [128, 512], F32, st) for i in range(2)]

                P.begin()
                R = lambda n="": Res(n)
                cc = P.chans(4, "cstF")
                r_f1, r_f2, r_f3, r_gmf = R(), R(), R(), R()
                P.dma("sp", f1[:], T["f1"], writes=[r_f1], chan=cc[0])
                P.dma("sp", f2[:], T["f2"], writes=[r_f2], chan=cc[1])
                P.dma("sp", f3[:], T["f3"], writes=[r_f3], chan=cc[2])
                P.dma("sp", gmf[:], pbcast(T["g_mix"][l, 0:256]), writes=[r_gmf], chan=cc[3])
                xflat = xf_d.rearrange("t p c -> t (p c)")
                tflat = t1_d.rearrange("r p c -> r (p c)")
                r_xin = [R(), R(), R()]
                r_t1s = [R(), R()]
                r_t1sd = [R(), R()]
                r_p1 = [RP(), RP()]
                r_p2 = [RP(), RP()]
                r_p3 = [RP(), RP()]
                r_t1d = [R() for _ in range(8)]
                cxin = P.chans(3, "xin")
                ct1 = P.chans(2, "t1s")
                nev = 0
                cT2 = P.chans(8, "T2", fresh=True)
                r_T2 = [R() for _ in range(8)]
                t1v = t1_d.rearrange("r p c -> p r c")
                NX = 3

                def ld(ci):
                    s3 = ci % NX
                    P.dma("sp", xin[s3][:], xflat[:, ci * 4096:(ci + 1) * 4096], writes=[r_xin[s3]], chan=cxin[s3])

                ld(0)
                ld(1)
                for ci in range(8):
                    s = ci % 2
                    s3 = ci % NX
                    if ci + 2 < 8:
                        ld(ci + 2)
                    for sub in range(8):
                        b = sub % 2
                        P.op("pe", lambda e, s3=s3, sub=sub, b=b: e.matmul(
                            out=p1[b][:], lhsT=f1[:], rhs=xin[s3][:, sub * 512:(sub + 1) * 512], start=True, stop=True),
                            reads=[r_f1, r_xin[s3]], writes=[r_p1[b]])
                        eng = "act" if nev % 2 == 0 else "dve"
                        nev += 1
                        if eng == "act":
                            P.op("act", lambda e, s=s, sub=sub, b=b: e.copy(out=t1s[s][:, sub * 512:(sub + 1) * 512], in_=p1[b][:]),
                                 reads=[r_p1[b]], writes=[r_t1s[s]])
                        else:
                            P.op("dve", lambda e, s=s, sub=sub, b=b: e.tensor_copy(out=t1s[s][:, sub * 512:(sub + 1) * 512], in_=p1[b][:]),
                                 reads=[r_p1[b]], writes=[r_t1sd[s]])
                    P.dma("sp", tflat[:, ci * 4096:(ci + 1) * 4096], t1s[s][:], reads=[r_t1s[s], r_t1sd[s]], writes=[r_t1d[ci]], chan=ct1[s])
                    P.dma("pool", T2[16 * ci:16 * ci + 16, :, :], t1v[16 * ci:16 * ci + 16, :, :], reads=[r_t1d[ci]],
                          writes=[r_T2[ci]], chan=cT2[ci])
                r_GT = R()
                r_GTd = R()
                GTv = [GT[:, j, :, :].rearrange("p r (kt kp) -> p r kt kp", kt=32) for j in range(2)]
                r_GTk = [[R(), R()] for _ in range(32)]
                r_ss, r_rs = [R(), R()], [R(), R()]
                r_junk = R()
                r_yfb = [R(), R()]
                cyf = P.chans(2, "yf")

                def s3(t):
                    b = t % 2
                    for j in range(2):
                        for ri in range(2):
                            P.op("pe", lambda e, t=t, j=j, ri=ri, b=b: e.matmul(
                                out=p3[b][:, j * 128:(j + 1) * 128], lhsT=GTv[j][:, ri, t, :],
                                rhs=f3[:, ri, :], start=(ri == 0), stop=(ri == 1)),
                                reads=[r_GTk[t][j], r_f3], writes=[r_p3[b]])
                    P.op("act", lambda e, b=b: e.activation(out=junk[:], in_=p3[b][:, 0:256], func=AF.Square, accum_out=ssF[b][:]),
                         reads=[r_p3[b]], writes=[r_junk, r_ss[b]])
                    P.op("act", lambda e, b=b: e.activation(out=rsF[b][:], in_=ssF[b][:], func=AF.Ln, scale=1.0 / 256, bias=epsb[:, 0:1]),
                         reads=[r_ss[b]], writes=[r_rs[b]])
                    P.op("act", lambda e, b=b: e.activation(out=rsF[b][:], in_=rsF[b][:], func=AF.Exp, scale=-0.5),
                         reads=[r_rs[b]], writes=[r_rs[b]])
                    P.op("dve", lambda e, b=b: e.scalar_tensor_tensor(out=yfb[b][:], in0=p3[b][:, 0:256], scalar=rsF[b][:, 0:1],
                                                                      in1=gmf[:], op0=ALU.mult, op1=ALU.mult),
                         reads=[r_p3[b], r_rs[b], r_gmf], writes=[r_yfb[b]])
                    P.dma("sp", y_d.rearrange("(kp kt) c -> kt kp c", kt=32)[t, :, 0:256], yfb[b][:], reads=[r_yfb[b]], chan=cyf[b])

                idx = 0
                for kt in range(32):
                    for j in range(2):
                        b = (idx // 2) % 2
                        hf = idx % 2
                        rr = r_T2
                        P.op("pe", lambda e, kt=kt, j=j, b=b, hf=hf: e.matmul(
                            out=p2[b][:, hf * 256:(hf + 1) * 256], lhsT=T2[:, kt, j * 128:(j + 1) * 128],
                            rhs=f2[:, kt, 1:3, :].rearrange("p a k -> p (a k)"), start=True, stop=False),
                            reads=rr + [r_f2], writes=[r_p2[b]])
                        P.op("pe", lambda e, kt=kt, j=j, b=b, hf=hf: e.matmul(
                            out=p2[b][:, hf * 256:(hf + 1) * 256], lhsT=T2[:, 32 + kt, j * 128:(j + 1) * 128],
                            rhs=f2[:, kt, 0:2, :].rearrange("p a k -> p (a k)"), start=False, stop=True),
                            reads=rr + [r_f2], writes=[r_p2[b]])
                        src_ = lambda b=b, hf=hf: p2[b][:, hf * 256:(hf + 1) * 256].rearrange("p (r k) -> p r k", r=2)
                        dst_ = lambda kt=kt, j=j: GTv[j][:, :, kt, :]
                        if idx % 2 == 0:
                            P.op("act", lambda e, src_=src_, dst_=dst_: e.copy(out=dst_(), in_=src_()),
                                 reads=[r_p2[b]], writes=[r_GTk[kt][j]])
                        else:
                            P.op("dve", lambda e, src_=src_, dst_=dst_: e.tensor_copy(out=dst_(), in_=src_()),
                                 reads=[r_p2[b]], writes=[r_GTk[kt][j]])
                        idx += 1
                    if kt >= 2:
                        s3(kt - 2)
                s3(30)
                s3(31)
                P.run_block("F")

        def phase_B(l, qT, kT, V, W):
            with contextlib.ExitStack() as st:
                NPT = 4
                NS = 3
                pt = [sb("pt%d" % i, [128, 1024], BF16, st) for i in range(NPT)]
                oT = [sb("oT%d" % i, [128, 512], F32, st) for i in range(2)]
                ya = [sb("ya%d" % i, [128, 4, 512], F32, st) for i in range(2)]
                rec = [sb("rec%d" % i, [128, 4], F32, st) for i in range(2)]
                junk = [sb("junkB%d" % i, [128, 512], F32, st) for i in range(2)]
                ssB = sb("ssB", [128, 4], F32, st)
                rsB = sb("rsB", [128, 4], F32, st)
                gma = sb("gma", [128, 512], F32, st)
                yab = [sb("yab%d" % i, [128, 512], BF16, st) for i in range(2)]
                psS = [ps("psS%d" % i, [128, 1024], F32, st) for i in range(NS)]
                psO = [ps("psO%d" % i, [128, 512], F32, st) for i in range(2)]

                P.begin()
                R = lambda n="": Res(n)
                cg = P.chan("gma")
                r_gma = R()
                P.dma("sp", gma[:], pbcast(T["g_mix"][l, 512:1024]), writes=[r_gma], chan=cg)
                cwp = P.chans(8, "wpre", fresh=True)
                for k2 in range(4):
                    P.dma("pool", W["g"][:, 2 * k2:2 * k2 + 2, :],
                          T["w_gate"][l, k2 * 256:(k2 + 1) * 256, :].rearrange("(k p) n -> p k n", p=128), chan=cwp[2 * k2])
                    P.dma("pool", W["u"][:, 2 * k2:2 * k2 + 2, :],
                          T["w_up"][l, k2 * 256:(k2 + 1) * 256, :].rearrange("(k p) n -> p k n", p=128), chan=cwp[2 * k2 + 1])
                cwo = P.chans(4, "w_o", fresh=True)
                for k2 in range(4):
                    P.dma("pool", W["o"][:, 2 * k2:2 * k2 + 2, :],
                          T["w_out"][l, k2 * 256:(k2 + 1) * 256, :].rearrange("(k p) n -> p k n", p=128), chan=cwo[k2])
                r_pt = [R() for _ in range(NPT)]
                r_psS = [RP() for _ in range(NS)]
                r_psO = [RP(), RP()]
                r_oT = [R(), R()]
                r_ya = [R(), R()]
                r_yab = [R(), R()]
                r_rec = [R(), R()]
                r_ss = R()
                r_rs = R()
                r_junk = [R(), R()]
                cya = P.chans(2, "ya")

                groups = [(qb, j, kt) for qb in range(8) for j in range(4) for kt in range(NT)]
                N = len(groups)
                pending_fin = []
                ring = [0]

                def emit_S(i):
                    qb, j, kt = groups[i]
                    sbi = ring[0] % NS
                    ring[0] += 1
                    for e_ in range(2):
                        P.op("pe", lambda e, qb=qb, j=j, e_=e_, kt=kt, sbi=sbi: e.matmul(
                            out=psS[sbi][:, e_ * 512:(e_ + 1) * 512], lhsT=kT[64 * e_:64 * e_ + 64, kt * 128:(kt + 1) * 128],
                            rhs=qT[64 * e_:64 * e_ + 64, j, qb * 512:(qb + 1) * 512], start=True, stop=True),
                            writes=[r_psS[sbi]])
                    pi = i % NPT
                    P.op("act", lambda e, sbi=sbi, pi=pi: e.activation(out=pt[pi][:], in_=psS[sbi][:], func=AF.Exp, scale=0.125),
                         reads=[r_psS[sbi]], writes=[r_pt[pi]])

                def emit_PV(i):
                    qb, j, kt = groups[i]
                    pi = i % NPT
                    for e_ in range(2):
                        P.op("pe", lambda e, e_=e_, kt=kt, pi=pi: e.matmul(
                            out=psO[e_][:, :], lhsT=V[:, kt, e_, :], rhs=pt[pi][:, e_ * 512:(e_ + 1) * 512],
                            start=(kt == 0), stop=(kt == NT - 1)),
                            reads=[r_pt[pi]], writes=[r_psO[e_]])
                    if kt == NT - 1:
                        yb = qb % 2
                        for e_ in range(2):
                            P.op("dve", lambda e, e_=e_: e.tensor_copy(out=oT[e_][:], in_=psO[e_][:, :]),
                                 reads=[r_psO[e_]], writes=[r_oT[e_]])

                        def fin(j=j, yb=yb, qb=qb, last=(j == 3)):
                            sbi = ring[0] % NS
                            ring[0] += 1
                            pv = psS[sbi][:].rearrange("p (b c) -> p b c", b=8)
                            for e_ in range(2):
                                for i4 in range(4):
                                    P.op("pe", lambda e, i4=i4, e_=e_, pv=pv: e.transpose(
                                        out=pv[:, e_ * 4 + i4, :], in_=oT[e_][:, i4 * 128:(i4 + 1) * 128], identity=identf[:]),
                                        reads=[r_oT[e_]], writes=[r_psS[sbi]])
                            for e_ in range(2):
                                h = 4 * e_ + j
                                P.op("dve", lambda e, e_=e_, pv=pv: e.reciprocal(out=rec[e_][:], in_=pv[:, e_ * 4:e_ * 4 + 4, 64]),
                                     reads=[r_psS[sbi]], writes=[r_rec[e_]])
                                P.op("dve", lambda e, e_=e_, h=h, yb=yb, pv=pv: e.tensor_tensor(
                                    out=ya[yb][:, :, h * 64:(h + 1) * 64], in0=pv[:, e_ * 4:e_ * 4 + 4, 0:64], in1=bcast(rec[e_][:], 2, 64), op=ALU.mult),
                                    reads=[r_psS[sbi], r_rec[e_]], writes=[r_ya[yb]])
                            if last:
                                pending_fin.append((i + 6, lambda yb=yb, qb=qb: fin2a(yb, qb)))
                                pending_fin.append((i + 10, lambda yb=yb, qb=qb: fin2b(yb, qb)))
                                pending_fin.sort(key=lambda x: x[0])

                        def fin2a(yb, qb):
                            for i4 in range(4):
                                jb = i4 % 2
                                P.op("pool", lambda e, yb=yb, i4=i4, jb=jb: e.tensor_tensor(out=junk[jb][:], in0=ya[yb][:, i4, :], in1=ya[yb][:, i4, :], op=ALU.mult),
                                     reads=[r_ya[yb]], writes=[r_junk[jb]])
                                P.op("dve", lambda e, i4=i4, jb=jb: e.tensor_reduce(out=ssB[:, i4:i4 + 1], in_=junk[jb][:], axis=AX.X, op=ALU.add),
                                     reads=[r_junk[jb]], writes=[r_ss])

                        def fin2b(yb, qb):
                            P.op("act", lambda e: e.activation(out=rsB[:], in_=ssB[:], func=AF.Ln, scale=1.0 / 512, bias=epsb[:, 0:1]),
                                 reads=[r_ss], writes=[r_rs])
                            P.op("act", lambda e: e.activation(out=rsB[:], in_=rsB[:], func=AF.Exp, scale=-0.5),
                                 reads=[r_rs], writes=[r_rs])
                            for i4 in range(4):
                                t = qb * 4 + i4
                                s2 = i4 % 2
                                P.op("dve", lambda e, yb=yb, i4=i4, s2=s2: e.scalar_tensor_tensor(
                                    out=yab[s2][:], in0=ya[yb][:, i4, :], scalar=rsB[:, i4:i4 + 1], in1=gma[:], op0=ALU.mult, op1=ALU.mult),
                                    reads=[r_ya[yb], r_rs, r_gma], writes=[r_yab[s2]])
                                P.dma("sp", y_d[t * 128:(t + 1) * 128, 512:1024], yab[s2][:], reads=[r_yab[s2]], chan=cya[s2])
                        pending_fin.append((i + 4, fin))
                        pending_fin.sort(key=lambda x: x[0])

                SK = 2
                for i in range(N + SK):
                    if i < N:
                        emit_S(i)
                    if i >= SK:
                        emit_PV(i - SK)
                    while pending_fin and pending_fin[0][0] <= i - SK:
                        pending_fin.pop(0)[1]()
                while pending_fin:
                    pending_fin.pop(0)[1]()
                P.run_block("B")

        def rr(gens):
            gens = list(gens)
            while gens:
                for g in list(gens):
                    try:
                        next(g)
                    except StopIteration:
                        gens.remove(g)

        def phase_C(l, src, W):
            w_o = W["o"]
            with contextlib.ExitStack() as st:
                gpm = sb("gpm", [128, D], F32, st)
                yt = [sb("yt%d" % i, [128, D], BF16, st) for i in range(3)]
                yT = [sb("yT%d" % i, [128, 8, 128], BF16, st) for i in range(2)]
                xs = [sb("xsC%d" % i, [128, D], F32, st) for i in range(3)]
                tmp = [sb("tmpC%d" % i, [128, D], F32, st) for i in range(2)]
                junk = sb("junkC", [128, D], BF16, st)
                ssC = [sb("ssC%d" % i, [128, 1], F32, st) for i in range(2)]
                rsC = [sb("rsC%d" % i, [128, 1], F32, st) for i in range(2)]
                pT = [ps("pTC%d" % i, [128, 8, 128], BF16, st) for i in range(2)]
                pm = [ps("pm%d" % i, [128, D], F32, st) for i in range(2)]

                P.begin()
                R = lambda n="": Res(n)
                r_w = [R() for _ in range(8)]
                cwd = P.chans(NFF // 2, "w_dpre", fresh=True)
                for f2 in range(NFF // 2):
                    P.dma("pool", W["d"][:, 2 * f2:2 * f2 + 2, :],
                          T["w_down"][l, f2 * 256:(f2 + 1) * 256, :].rearrange("(k p) n -> p k n", p=128), chan=cwd[f2])
                cgp = P.chan("gpm")
                r_gpm = R()
                P.dma("sp", gpm[:], pbcast(T["g_post_mix"][l]), writes=[r_gpm], chan=cgp)
                r_yT, r_yTd, r_tmp, r_ss, r_rs = [[R(), R()] for _ in range(5)]
                r_yt = [R() for _ in range(3)]
                r_xs = [R() for _ in range(3)]
                r_pT, r_pm = [[RP(), RP()] for _ in range(2)]
                r_junk = R()
                cy = P.chans(3, "ytl")
                cx = P.chans(3, "xsl")
                cxo = P.chans(3, "xso")

                def c_load(t):
                    y3 = t % 3
                    rows = slice(t * 128, (t + 1) * 128)
                    P.dma("sp", yt[y3][:], y_d[rows, :], writes=[r_yt[y3]], chan=cy[y3])
                    yield

                def c_loadx(t):
                    x3 = t % 3
                    rows = slice(t * 128, (t + 1) * 128)
                    P.dma("sp", xs[x3][:], src[rows, :], writes=[r_xs[x3]], chan=cx[x3])
                    yield

                def c_fa(t):
                    s = t % 2
                    y3 = t % 3
                    for j in range(8):
                        P.op("pe", lambda e, j=j: e.transpose(out=pT[s][:, j, :], in_=yt[y3][:, j * 128:(j + 1) * 128], identity=ident[:]),
                             reads=[r_yt[y3]], writes=[r_pT[s]])
                    yield
                    P.op("act", lambda e: e.copy(out=yT[s][:, 0:4, :], in_=pT[s][:, 0:4, :]), reads=[r_pT[s]], writes=[r_yT[s]])
                    yield
                    P.op("dve", lambda e: e.tensor_copy(out=yT[s][:, 4:8, :], in_=pT[s][:, 4:8, :]), reads=[r_pT[s]], writes=[r_yTd[s]])
                    yield

                def c_fb(t):
                    s = t % 2
                    for n in range(2):
                        for kc in range(8):
                            P.op("pe", lambda e, n=n, kc=kc: e.matmul(
                                out=pm[s][:, n * 512:(n + 1) * 512], lhsT=yT[s][:, kc, :], rhs=w_o[:, kc, n * 512:(n + 1) * 512],
                                start=(kc == 0), stop=(kc == 7)), reads=[r_yT[s], r_yTd[s], r_w[kc]], writes=[r_pm[s]])
                        yield

                def c_back(t):
                    s = t % 2
                    x3 = t % 3
                    rows = slice(t * 128, (t + 1) * 128)
                    P.op("act", lambda e: e.activation(out=junk[:], in_=pm[s][:], func=AF.Square, accum_out=ssC[s][:]),
                         reads=[r_pm[s]], writes=[r_junk, r_ss[s]])
                    yield
                    P.op("act", lambda e: e.activation(out=rsC[s][:], in_=ssC[s][:], func=AF.Ln, scale=1.0 / D, bias=epsb[:, 0:1]),
                         reads=[r_ss[s]], writes=[r_rs[s]])
                    yield
                    P.op("act", lambda e: e.activation(out=rsC[s][:], in_=rsC[s][:], func=AF.Exp, scale=-0.5), reads=[r_rs[s]], writes=[r_rs[s]])
                    yield
                    P.op("dve", lambda e: e.scalar_tensor_tensor(out=tmp[s][:], in0=pm[s][:], scalar=rsC[s][:, 0:1], in1=gpm[:],
                                                                 op0=ALU.mult, op1=ALU.mult),
                         reads=[r_pm[s], r_rs[s], r_gpm], writes=[r_tmp[s]])
                    yield
                    P.op("pool", lambda e: e.tensor_tensor(out=xs[x3][:], in0=xs[x3][:], in1=tmp[s][:], op=ALU.add),
                         reads=[r_xs[x3], r_tmp[s]], writes=[r_xs[x3]])
                    yield
                    P.dma("sp", out[rows, :], xs[x3][:], reads=[r_xs[x3]], chan=cxo[x3])
                    yield

                rr([c_load(0), c_load(1), c_load(2), c_loadx(0), c_loadx(1)])
                rr([c_fa(0)])
                rr([c_fa(1), c_fb(0)])
                for t in range(NT):
                    gens = [c_back(t)]
                    if t + 1 < NT:
                        gens.insert(0, c_fb(t + 1))
                    if t + 2 < NT:
                        gens.insert(0, c_fa(t + 2))
                        gens.insert(0, c_loadx(t + 2))
                    if t + 3 < NT:
                        gens.insert(0, c_load(t + 3))
                    rr(gens)
                P.run_block("C")

        def phase_D(l, W):
            G = 2
            NG = NT // G
            GW = 128 * G
            w_g, w_u, w_d = W["g"], W["u"], W["d"]
            with contextlib.ExitStack() as st:
                gpf = sb("gpf", [128, D], F32, st)
                gpo = sb("gpo", [128, D], F32, st)
                xs = [sb("xsD%d" % i, [128, G, D], F32, st) for i in range(2)]
                h2 = [sb("h2_%d" % i, [128, D], BF16, st) for i in range(2)]
                h2T = sb("h2T", [128, 8, GW], BF16, st)
                sg = [sb("sg%d" % i, [128, GW], F32, st) for i in range(2)]
                aT = sb("aT", [128, NFF, GW], BF16, st)
                tmp = sb("tmpD", [128, D], F32, st)
                junk = sb("junkD", [128, D], BF16, st)
                ssD = sb("ssD", [128, 2], F32, st)
                rsD = sb("rsD", [128, 2], F32, st)
                ssE = [sb("ssE%d" % i, [128, 1], F32, st) for i in range(2)]
                rsE = [sb("rsE%d" % i, [128, 1], F32, st) for i in range(2)]
                pT = [ps("pTD%d" % i, [128, 8, 128], BF16, st) for i in range(2)]
                pg = [ps("pg%d" % i, [128, 512], F32, st) for i in range(2)]
                pu = [ps("pu%d" % i, [128, 512], F32, st) for i in range(2)]
                pf = ps("pf", [128, D], F32, st)

                P.begin()
                R = lambda n="": Res(n)
                cgg = P.chans(2, "gD")
                r_gpf, r_gpo = R(), R()
                P.dma("sp", gpf[:], pbcast(T["g_pre_ffn"][l]), writes=[r_gpf], chan=cgg[0])
                P.dma("sp", gpo[:], pbcast(T["g_post_ffn"][l]), writes=[r_gpo], chan=cgg[1])
                r_xs = [[R() for _ in range(G)] for _ in range(2)]
                r_sg, r_h2, r_ssE, r_rsE = [[R(), R()] for _ in range(4)]
                r_ssD, r_rsD = R(), R()
                r_pg, r_pu = [[RP(), RP()] for _ in range(2)]
                r_h2T, r_aT, r_tmp, r_junk = [R() for _ in range(4)]
                r_pT, r_pfh = [RP(), RP()], [RP(), RP()]
                cx = [P.chans(G, "xsD%d" % i) for i in range(2)]
                cxo = [P.chans(G, "xoD%d" % i) for i in range(2)]

                def d_load(gi):
                    s = gi % 2
                    for i in range(G):
                        t = gi * G + i
                        P.dma("sp", xs[s][:, i, :], out[t * 128:(t + 1) * 128, :], writes=[r_xs[s][i]], chan=cx[s][i])
                        yield

                def d_front_a(gi):
                    s = gi % 2
                    for i in range(G):
                        P.op("act", lambda e, i=i: e.activation(out=junk[:], in_=xs[s][:, i, :], func=AF.Square, accum_out=ssD[:, i:i + 1]),
                             reads=[r_xs[s][i]], writes=[r_junk, r_ssD])
                        yield
                    P.op("act", lambda e: e.activation(out=rsD[:], in_=ssD[:], func=AF.Ln, scale=1.0 / D, bias=epsb[:, 0:1]),
                         reads=[r_ssD], writes=[r_rsD])
                    P.op("act", lambda e: e.activation(out=rsD[:], in_=rsD[:], func=AF.Exp, scale=-0.5),
                         reads=[r_rsD], writes=[r_rsD])
                    yield
                    for i in range(G):
                        P.op("dve", lambda e, i=i: e.scalar_tensor_tensor(out=h2[i][:], in0=xs[s][:, i, :], scalar=rsD[:, i:i + 1], in1=gpf[:],
                                                                          op0=ALU.mult, op1=ALU.mult),
                             reads=[r_xs[s][i], r_rsD, r_gpf], writes=[r_h2[i]])
                        yield

                def d_front_b(gi):
                    for i in range(G):
                        for j in range(8):
                            P.op("pe", lambda e, i=i, j=j: e.transpose(out=pT[i][:, j, :], in_=h2[i][:, j * 128:(j + 1) * 128], identity=ident[:]),
                                 reads=[r_h2[i]], writes=[r_pT[i]])
                        yield
                    for i in range(G):
                        P.op("act", lambda e, i=i: e.copy(out=h2T[:, :, i * 128:(i + 1) * 128], in_=pT[i][:]),
                             reads=[r_pT[i]], writes=[r_h2T])
                        yield

                fi = [0]

                def d_gateup(gi):
                    for f in range(NFF):
                        b = fi[0] % 2
                        fi[0] += 1
                        for kc in range(8):
                            P.op("pe", lambda e, f=f, kc=kc, b=b: e.matmul(
                                out=pg[b][:, 0:GW], lhsT=w_g[:, kc, f * 128:(f + 1) * 128], rhs=h2T[:, kc, :],
                                start=(kc == 0), stop=(kc == 7)), reads=[r_h2T], writes=[r_pg[b]])
                        for kc in range(8):
                            P.op("pe", lambda e, f=f, kc=kc, b=b: e.matmul(
                                out=pu[b][:, 0:GW], lhsT=w_u[:, kc, f * 128:(f + 1) * 128], rhs=h2T[:, kc, :],
                                start=(kc == 0), stop=(kc == 7)), reads=[r_h2T], writes=[r_pu[b]])
                        P.op("act", lambda e, b=b: e.activation(out=sg[b][:], in_=pg[b][:, 0:GW], func=AF.Silu),
                             reads=[r_pg[b]], writes=[r_sg[b]])
                        P.op("dve", lambda e, f=f, b=b: e.tensor_tensor(out=aT[:, f, :], in0=sg[b][:], in1=pu[b][:, 0:GW], op=ALU.mult),
                             reads=[r_sg[b], r_pu[b]], writes=[r_aT])
                        yield

                def d_down(gi, i):
                    s = gi % 2
                    t = gi * G + i
                    rows = slice(t * 128, (t + 1) * 128)
                    for n in range(2):
                        for f in range(NFF):
                            P.op("pe", lambda e, n=n, f=f: e.matmul(
                                out=pf[:, n * 512:(n + 1) * 512], lhsT=aT[:, f, i * 128:(i + 1) * 128],
                                rhs=w_d[:, f, n * 512:(n + 1) * 512], start=(f == 0), stop=(f == NFF - 1)),
                                reads=[r_aT], writes=[r_pfh[n]])
                        yield
                    for n in range(2):
                        P.op("dve", lambda e, n=n: e.tensor_copy(out=tmp[:, n * 512:(n + 1) * 512], in_=pf[:, n * 512:(n + 1) * 512]),
                             reads=[r_pfh[n]], writes=[r_tmp])
                    yield
                    P.op("act", lambda e: e.activation(out=junk[:], in_=tmp[:], func=AF.Square, accum_out=ssE[i][:]),
                         reads=[r_tmp], writes=[r_junk, r_ssE[i]])
                    yield
                    P.op("act", lambda e: e.activation(out=rsE[i][:], in_=ssE[i][:], func=AF.Ln, scale=1.0 / D, bias=epsb[:, 0:1]),
                         reads=[r_ssE[i]], writes=[r_rsE[i]])
                    yield
                    P.op("act", lambda e: e.activation(out=rsE[i][:], in_=rsE[i][:], func=AF.Exp, scale=-0.5), reads=[r_rsE[i]], writes=[r_rsE[i]])
                    yield
                    P.op("dve", lambda e: e.scalar_tensor_tensor(out=tmp[:], in0=tmp[:], scalar=rsE[i][:, 0:1], in1=gpo[:],
                                                                 op0=ALU.mult, op1=ALU.mult),
                         reads=[r_tmp, r_rsE[i], r_gpo], writes=[r_tmp])
                    yield
                    P.op("pool", lambda e: e.tensor_tensor(out=xs[s][:, i, :], in0=xs[s][:, i, :], in1=tmp[:], op=ALU.add),
                         reads=[r_xs[s][i], r_tmp], writes=[r_xs[s][i]])
                    yield
                    P.dma("sp", out[rows, :], xs[s][:, i, :], reads=[r_xs[s][i]], chan=cxo[s][i])
                    yield

                def seq(*gens):
                    for g in gens:
                        yield from g

                def delayed(n, gen):
                    for _ in range(n):
                        yield
                    yield from gen

                rr([d_load(0)])
                rr([d_front_a(0), d_load(1)])
                rr([d_front_b(0)])
                for gi in range(NG):
                    gens = [d_gateup(gi)]
                    if gi + 1 < NG:
                        gens.append(delayed(6, d_front_a(gi + 1)))
                    rr(gens)
                    rr([d_down(gi, 0)])
                    if gi + 1 < NG:
                        rr([d_front_b(gi + 1)])
                    rr([d_down(gi, 1)])
                    if gi + 2 < NG:
                        rr([d_load(gi + 2)])
                P.run_block("D")

        phase_setup()
        src = T["x"]
        for l in range(n_layers):
            if not layer(l, src):
                break
            src = out
    return nc


_CONSTS = None


def kernel(**inputs):
    global _CONSTS
    if _CONSTS is None:
        _CONSTS = make_consts()
    nc = build()
    x = np.ascontiguousarray(inputs["x"], dtype=np.float32)
    in_maps = []
    for b in range(8):
        m = {"x": x[b]}
        for k in PARAM_SHAPES:
            m[k] = np.ascontiguousarray(inputs[k], dtype=np.float32)
        m.update(_CONSTS)
        in_maps.append(m)
    res = run_bass_kernel_spmd(nc, in_maps, core_ids=list(range(8)))
    return np.stack([r["out"] for r in res.results], 0)
```

```python
import contextlib
import numpy as np
import ml_dtypes
import concourse.bass as bass
import concourse.mybir as mybir
from concourse.bass_utils import run_bass_kernel_spmd

F32 = mybir.dt.float32
BF16 = mybir.dt.bfloat16
ALU = mybir.AluOpType
AF = mybir.ActivationFunctionType
AX = mybir.AxisListType

S = 4096
D = 1024
NT = 32
DIN = 1536
DFF = 2816
NFF = 22
EPS = 1e-6
DEPTH = 2

import os
SAME_ENG_SYNC = os.environ.get('SES', '1') == '1'


class Res:
    __slots__ = ("name", "w", "rs", "excl")

    def __init__(self, name="", excl=False):
        self.name = name
        self.w = None
        self.rs = []
        self.excl = excl


def RP():
    return Res("psum", True)


class Op:
    __slots__ = ("eng", "fn", "deps", "signal", "val", "sem", "is_dma", "name")


class Chan:
    def __init__(self, sem, name):
        self.sem = sem
        self.count = 0
        self.last = None
        self.name = name
        self.eng = None
        self.fresh = False


ENGS = ("pe", "act", "dve", "pool", "sp")


class Prog:
    def __init__(self, nc, stack):
        self.nc = nc
        self.stack = stack
        self.sems = {e: stack.enter_context(nc.semaphore("prog_" + e)) for e in ENGS}
        self.counts = {e: 0 for e in ENGS}
        self.free_chans = []
        self.nchan = 0
        self.ops = None
        self.phase_chans = None
        self.scopes = False

    def chan(self, name="c", fresh=False):
        if fresh:
            sem = self.stack.enter_context(self.nc.semaphore("fch%d" % self.nchan))
            self.nchan += 1
            c = Chan(sem, name)
            c.fresh = True
            self.phase_chans.append(c)
            return c
        if self.free_chans:
            c = self.free_chans.pop()
            c.name = name
            c.last = None
            c.eng = None
        else:
            sem = self.stack.enter_context(self.nc.semaphore("ch%d" % self.nchan))
            self.nchan += 1
            c = Chan(sem, name)
        self.phase_chans.append(c)
        return c

    def chans(self, n, name="c", fresh=False):
        return [self.chan("%s%d" % (name, i), fresh) for i in range(n)]

    def begin(self):
        self.ops = {e: [] for e in ENGS}
        self.phase_chans = []

    def _mk(self, eng, fn, reads, writes, is_dma, name):
        o = Op()
        o.eng = eng
        o.fn = fn
        o.signal = is_dma
        o.val = None
        o.sem = None
        o.is_dma = is_dma
        o.name = name
        deps = []
        seen = set()

        def add(d):
            if d is None or id(d) in seen:
                return
            seen.add(id(d))
            if d.eng == eng and not d.is_dma and not is_dma:
                if eng == "pe" or not SAME_ENG_SYNC:
                    return
            deps.append(d)

        for r in reads:
            add(r.w)
            if r.excl:
                for x in r.rs:
                    if x.eng != eng:
                        add(x)
        for w in writes:
            add(w.w)
            for x in w.rs:
                add(x)
        for r in reads:
            if not is_dma:
                r.rs = [x for x in r.rs if not (x.eng == eng and not x.is_dma)]
            r.rs.append(o)
        for w in writes:
            w.w = o
            w.rs = []
        o.deps = deps
        for d in deps:
            d.signal = True
        self.ops[eng].append(o)
        return o

    def op(self, eng, fn, reads=(), writes=(), name=""):
        return self._mk(eng, fn, reads, writes, False, name)

    def dma(self, eng, out, in_, reads=(), writes=(), chan=None, name="", **kw):
        assert chan is not None
        if eng == "pool":
            assert chan.fresh and chan.count == 0, "pool DMA needs a fresh chan"

        def fn(e):
            return e.dma_start(out=out, in_=in_, **kw)

        o = self._mk(eng, fn, reads, writes, True, name)
        if chan.last is not None:
            o.deps.append(chan.last)
        assert chan.eng in (None, eng)
        chan.eng = eng
        chan.count += 1
        chan.last = o
        o.sem = chan.sem
        o.val = 16 * chan.count
        return o

    def finalize(self):
        for e in ENGS:
            cnt = self.counts[e]
            for o in self.ops[e]:
                if o.is_dma:
                    continue
                if o.signal:
                    cnt += 1
                    o.val = cnt
                    o.sem = self.sems[e]
            self.counts[e] = cnt

    def emit(self, engname, e):
        waited = {}
        for o in self.ops[engname]:
            need = {}
            for d in o.deps:
                assert d.val is not None, (d.name, o.name)
                if need.get(d.sem, 0) < d.val:
                    need[d.sem] = d.val
            for s, v in need.items():
                if waited.get(s, 0) >= v:
                    continue
                e.wait_ge(s, v)
                waited[s] = v
            ins = o.fn(e)
            if o.signal:
                ins.then_inc(o.sem, 16 if o.is_dma else 1)
        for c in self.phase_chans:
            if c.eng == engname and c.count > 0:
                e.wait_ge(c.sem, 16 * c.count)

    def run_block(self, name=None):
        self.finalize()
        nc = self.nc
        self.nblk = getattr(self, "nblk", 0) + 1
        scope = nc.named_scope("%02d_%s" % (self.nblk, name or "blk")) if self.scopes else contextlib.nullcontext()
        with scope, nc.Block() as block:
            @block.tensor
            def _(e):
                self.emit("pe", e)

            @block.scalar
            def _(e):
                self.emit("act", e)

            @block.vector
            def _(e):
                self.emit("dve", e)

            @block.gpsimd
            def _(e):
                self.emit("pool", e)

            @block.sync
            def _(e):
                self.emit("sp", e)
        self.free_chans.extend(c for c in self.phase_chans if not c.fresh)
        self.phase_chans = None
        self.ops = None


def delayed(n, gen):
    for _ in range(n):
        yield
    yield from gen


def bcast(ap, axis, n):
    lst = [list(x) for x in ap.ap]
    lst.insert(axis, [0, n])
    return bass.AP(ap.tensor, ap.offset, lst)


def pbcast(ap, n=128):
    lst = [[0, n]] + [list(x) for x in ap.ap]
    return bass.AP(ap.tensor, ap.offset, lst)


def _bf(a):
    return np.ascontiguousarray(a.astype(np.float32)).astype(ml_dtypes.bfloat16)


def make_consts():
    c = {}
    c["ident"] = _bf(np.eye(128))
    c["identf"] = np.eye(128, dtype=np.float32)
    tok = np.arange(S)
    row = (tok // 64).astype(np.float32)
    col = (tok % 64).astype(np.float32)
    freqs = (np.float32(10000.0) ** (-np.arange(16, dtype=np.float32) / np.float32(16))).astype(np.float32)
    ar = (row[:, None] * freqs).astype(np.float32)
    ac = (col[:, None] * freqs).astype(np.float32)
    cr, sr, cc, sc = np.cos(ar), np.sin(ar), np.cos(ac), np.sin(ac)
    COS = np.concatenate([cr, cr, cc, cc], 1)
    SINS = np.concatenate([-sr, sr, -sc, sc], 1)
    c["cos"] = np.ascontiguousarray(COS.reshape(NT, 128, 64).transpose(1, 0, 2)).astype(np.float32)
    c["sins"] = np.ascontiguousarray(SINS.reshape(NT, 128, 64).transpose(1, 0, 2)).astype(np.float32)
    t = np.arange(32)
    ang = 2 * np.pi * np.outer(t, t) / 32.0
    c["f1"] = _bf(np.concatenate([np.cos(ang), -np.sin(ang)], 1))
    p = np.arange(128)[:, None, None]
    kt = np.arange(32)[None, :, None]
    kp = np.arange(128)[None, None, :]
    th = 2 * np.pi * ((p * (kt + 32 * kp)) % 4096) / 4096.0
    c["f2"] = _bf(np.stack([np.sin(th), np.cos(th), -np.sin(th)], 2))
    cidx = np.arange(64)
    a64 = 2 * np.pi * np.outer(cidx, cidx) / 64.0
    C64 = np.cos(a64) / 512.0
    S64 = np.sin(a64) / 512.0
    Cb = np.zeros((128, 128)); Sb = np.zeros((128, 128))
    for g in range(2):
        Cb[64 * g:64 * g + 64, 64 * g:64 * g + 64] = C64
        Sb[64 * g:64 * g + 64, 64 * g:64 * g + 64] = S64
    c["f3"] = _bf(np.stack([Cb, Sb], 1))
    return c


CONST_SHAPES = {
    "ident": ([128, 128], BF16), "identf": ([128, 128], F32),
    "cos": ([128, NT, 64], F32), "sins": ([128, NT, 64], F32),
    "f1": ([32, 64], BF16), "f2": ([128, 32, 3, 128], BF16), "f3": ([128, 2, 128], BF16),
}

PARAM_SHAPES = {
    "g_pre_mix": [DEPTH, D], "w_in": [DEPTH, D, DIN], "sgu_w": [DEPTH, 4, 128, 128],
    "sgu_b": [DEPTH, 4, 128], "sgu_g": [DEPTH, 256], "g_q": [DEPTH, 64], "g_k": [DEPTH, 64],
    "g_mix": [DEPTH, D], "w_out": [DEPTH, D, D], "g_post_mix": [DEPTH, D],
    "g_pre_ffn": [DEPTH, D], "w_gate": [DEPTH, D, DFF], "w_up": [DEPTH, D, DFF],
    "w_down": [DEPTH, DFF, D], "g_post_ffn": [DEPTH, D],
}


def build(n_layers=DEPTH, stop_after=None, debug=False):
    nc = bass.Bass("TRN2", target_bir_lowering=False)
    T = {}
    T["x"] = nc.dram_tensor("x", [S, D], F32, kind="ExternalInput").ap()
    for k, shp in PARAM_SHAPES.items():
        T[k] = nc.dram_tensor(k, shp, F32, kind="ExternalInput").ap()
    for k, (shp, dt) in CONST_SHAPES.items():
        T[k] = nc.dram_tensor(k, shp, dt, kind="ExternalInput").ap()
    out = nc.dram_tensor("out", [S, D], F32, kind="ExternalOutput").ap()
    kind_dbg = "ExternalOutput" if debug else "Internal"
    xf_d = nc.dram_tensor("xf_d", [NT, 128, 256], BF16, kind=kind_dbg).ap()
    t1_d = nc.dram_tensor("t1_d", [64, 128, 256], BF16, kind=kind_dbg).ap()
    y_d = nc.dram_tensor("y_d", [S, D], BF16, kind=kind_dbg).ap()
    dbg = {}
    if debug:
        dbg["qT"] = nc.dram_tensor("dbg_qT", [128, 4, S], BF16, kind="ExternalOutput").ap()
        dbg["kT"] = nc.dram_tensor("dbg_kT", [128, S], BF16, kind="ExternalOutput").ap()
        dbg["V"] = nc.dram_tensor("dbg_V", [128, NT, 2, 128], BF16, kind="ExternalOutput").ap()

    with contextlib.ExitStack() as stack:
        P = Prog(nc, stack)
        P.scopes = False

        uid = [0]

        def sb(name, shape, dt, st=None, side=None):
            uid[0] += 1
            return (st or stack).enter_context(nc.sbuf_tensor("s%d_%s" % (uid[0], name), shape, dt, side=side))

        def ps(name, shape, dt, st):
            uid[0] += 1
            return st.enter_context(nc.psum_tensor("p%d_%s" % (uid[0], name), shape, dt))

        ident = sb("ident", [128, 128], BF16)
        identf = sb("identf", [128, 128], F32)
        epsb = sb("epsb", [128, 1], F32)

        def layer(l, src):
            with contextlib.ExitStack() as wst:
                W = {}
                with contextlib.ExitStack() as lst:
                    qT = sb("qT", [128, 4, S], BF16, lst)
                    kT = sb("kT", [128, S], BF16, lst)
                    V = sb("V", [128, NT, 2, 128], BF16, lst)
                    phase_A(l, src, qT, kT, V)
                    if stop_after == "A":
                        if debug:
                            dump(qT, kT, V)
                        return False
                    phase_F(l)
                    if stop_after == "F":
                        return False
                    W["g"] = sb("w_g", [128, 8, DFF], BF16, wst, side="right")
                    W["u"] = sb("w_u", [128, 8, DFF], BF16, wst, side="right")
                    W["o"] = sb("w_o", [128, 8, D], BF16, wst, side="right")
                    phase_B(l, qT, kT, V, W)
                    if stop_after == "B":
                        return False
                W["d"] = sb("w_d", [128, NFF, D], BF16, wst, side="right")
                phase_C(l, src, W)
                if stop_after == "C":
                    return False
                phase_D(l, W)
            return True

        def dump(qT, kT, V):
            P.begin()
            c = P.chans(3, "dump")
            P.dma("sp", dbg["qT"], qT[:], chan=c[0])
            P.dma("sp", dbg["kT"], kT[:], chan=c[1])
            P.dma("sp", dbg["V"], V[:], chan=c[2])
            P.run_block()

        def phase_setup():
            P.begin()
            c = P.chans(2, "setup")
            r1, r2 = Res(), Res()
            P.dma("sp", ident[:], T["ident"], writes=[r1], chan=c[0])
            P.dma("sp", identf[:], T["identf"], writes=[r2], chan=c[1])
            P.op("dve", lambda e: e.memset(epsb[:], EPS))
            P.run_block()

        def phase_A(l, src, qT, kT, V):
            with contextlib.ExitStack() as st:
                w_in = sb("w_in", [128, 8, DIN], BF16, st)
                cos = sb("cos", [128, NT, 64], F32, st)
                sins = sb("sins", [128, NT, 64], F32, st)
                gpp = sb("gpp", [128, 8], F32, st)
                gv = sb("gv", [128, 256], F32, st)
                gms = sb("gms", [128, 256], F32, st)
                gtab = sb("gtab", [128, 10, 64], F32, st)
                ws_f = sb("ws_f", [128, 4, 128], F32, st)
                ws_b = sb("ws_b", [128, 4, 128], BF16, st)
                wsT = sb("wsT", [128, 4, 128], BF16, st)
                bpp = sb("bpp", [128, 4], F32, st)
                junk = sb("junk", [128, D], BF16, st)
                junk2 = sb("junk2", [128, 256], BF16, st)
                two = lambda name, shape, dt: [sb("%s%d" % (name, i), shape, dt, st) for i in range(2)]
                xs = two("xs", [128, D], F32)
                ss = two("ss", [128, 1], F32)
                rstd = two("rstd", [128, 1], F32)
                h = two("h", [128, D], BF16)
                hT = two("hT", [128, 8, 128], BF16)
                zf = two("zf", [128, 256], BF16)
                ug = two("ug", [128, 256], F32)
                vg = two("vg", [128, 256], F32)
                bst = two("bst", [128, 6], F32)
                mv = two("mv", [128, 2], F32)
                rsv = two("rsv", [128, 1], F32)
                vn = two("vn", [128, 256], F32)
                vnb = two("vnb", [128, 256], BF16)
                ys = two("ys", [128, 256], F32)
                ss2 = two("ss2", [128, 1], F32)
                rs2 = two("rs2", [128, 1], F32)
                ysb = two("ysb", [128, 256], BF16)
                qk = two("qk", [128, 640], F32)
                qa = two("qa", [128, 640], F32)
                qb = two("qb", [128, 640], F32)
                qss = two("qss", [128, 10], F32)
                qrs = two("qrs", [128, 10], F32)
                qkr = two("qkr", [128, 640], BF16)
                pT = [ps("pT%d" % i, [128, 8, 128], BF16, st) for i in range(2)]
                pz = [ps("pz%d" % i, [128, 512], F32, st) for i in range(3)]
                psv = [ps("psv%d" % i, [128, 512], F32, st) for i in range(2)]
                pT2 = ps("pT2", [128, 8, 128], BF16, st)

                P.begin()
                R = lambda n="": Res(n)
                R2 = lambda: [Res(), Res()]
                r_w = [R("w%d" % i) for i in range(8)]
                cw = P.chans(3, "w_in", fresh=True)
                r_wc = [R("wc%d" % i) for i in range(3)]
                for c3 in range(3):
                    P.dma("pool", w_in[:, :, c3 * 512:(c3 + 1) * 512],
                          T["w_in"][l, :, c3 * 512:(c3 + 1) * 512].rearrange("(k p) n -> p k n", p=128),
                          writes=[r_wc[c3]], chan=cw[c3])
                cc = P.chans(12, "cst")
                r_cos, r_sins, r_gpre, r_gv, r_gms, r_gtab, r_wsf, r_bpp = [R() for _ in range(8)]
                P.dma("sp", gpp[:], T["g_pre_mix"][l].rearrange("(k p) -> p k", p=128), writes=[r_gpre], chan=cc[2],
                      allow_slow_non_contiguous=True)
                r_w2 = [R("w2_%d" % i) for i in range(3)]
                for c3 in range(3):
                    for kc in range(8):
                        P.op("dve", lambda e, kc=kc, c3=c3: e.tensor_scalar(
                            out=w_in[:, kc, c3 * 512:(c3 + 1) * 512], in0=w_in[:, kc, c3 * 512:(c3 + 1) * 512],
                            scalar1=gpp[:, kc:kc + 1], scalar2=None, op0=ALU.mult),
                            reads=[r_wc[c3], r_gpre], writes=[r_w2[c3]])
                P.dma("sp", ws_f[:], T["sgu_w"][l].rearrange("h p q -> p h q"), writes=[r_wsf], chan=cc[6])
                P.dma("sp", cos[:], T["cos"], writes=[r_cos], chan=cc[0])
                P.dma("sp", sins[:], T["sins"], writes=[r_sins], chan=cc[1])
                P.dma("sp", gv[:], pbcast(T["sgu_g"][l]), writes=[r_gv], chan=cc[3])
                P.dma("sp", gms[:], pbcast(T["g_mix"][l, 256:512]), writes=[r_gms], chan=cc[4])
                for hh in range(10):
                    srcg = T["g_q"][l] if hh < 8 else T["g_k"][l]
                    P.dma("sp", gtab[:, hh, :], pbcast(srcg), writes=[r_gtab], chan=cc[5])
                P.dma("sp", bpp[:], T["sgu_b"][l].rearrange("h p -> p h"), writes=[r_bpp], chan=cc[7],
                      allow_slow_non_contiguous=True)
                r_wsb, r_wsT = R(), R()
                r_pT = [RP(), RP()]
                r_psv = [RP(), RP()]
                P.op("dve", lambda e: e.tensor_copy(out=ws_b[:], in_=ws_f[:]), reads=[r_wsf], writes=[r_wsb])
                for hh in range(4):
                    P.op("pe", lambda e, hh=hh: e.transpose(out=pT[0][:, hh, :], in_=ws_b[:, hh, :], identity=ident[:]),
                         reads=[r_wsb], writes=[r_pT[0]])
                P.op("dve", lambda e: e.tensor_copy(out=wsT[:], in_=pT[0][:, 0:4, :]), reads=[r_pT[0]], writes=[r_wsT])
                r_V = R("V")
                P.op("pool", lambda e: e.memset(V[:, :, :, 64:128], 1.0), writes=[r_V])

                r_xs, r_ss, r_rstd, r_h, r_hT, r_zf, r_ysb, r_qkr = [R2() for _ in range(8)]
                r_ug, r_vg, r_bst, r_mv, r_rsv, r_vn, r_vnb, r_ys, r_ss2, r_rs2 = [R2() for _ in range(10)]
                r_qk, r_qa, r_qb, r_qss, r_qrs = [R2() for _ in range(5)]
                r_junk, r_junk2, r_pT2 = R(), R(), RP()
                r_hf = R2()
                r_pz = [RP() for _ in range(3)]
                r_qT, r_kT = R(), R()
                cx = P.chans(2, "xs")
                czf = P.chans(2, "zf")
                cys = P.chans(2, "ys")
                qk3 = lambda a: a[:].rearrange("p (h d) -> p h d", h=10)
                qk5 = lambda a: a[:].rearrange("p (h f x d) -> p (h f) x d", h=10, f=2, x=2)
                sin4 = lambda t: sins[:, t, :].rearrange("p (f x d) -> p f x d", f=2, x=2)

                def ch_load(t):
                    s = t % 2
                    P.dma("sp", xs[s][:], src[t * 128:(t + 1) * 128, :], writes=[r_xs[s]], chan=cx[s])
                    yield

                def ch_front(t):
                    s = t % 2
                    P.op("act", lambda e: e.activation(out=junk[:], in_=xs[s][:], func=AF.Square, accum_out=ss[s][:]),
                         reads=[r_xs[s]], writes=[r_junk, r_ss[s]])
                    yield
                    P.op("act", lambda e: e.activation(out=rstd[s][:], in_=ss[s][:], func=AF.Ln, scale=1.0 / D,
                                                       bias=epsb[:, 0:1]), reads=[r_ss[s]], writes=[r_rstd[s]])
                    yield
                    P.op("act", lambda e: e.activation(out=rstd[s][:], in_=rstd[s][:], func=AF.Exp, scale=-0.5),
                         reads=[r_rstd[s]], writes=[r_rstd[s]])
                    yield
                    P.op("act", lambda e: e.activation(out=h[s][:], in_=xs[s][:], func=AF.Copy, scale=rstd[s][:, 0:1]),
                         reads=[r_xs[s], r_rstd[s]], writes=[r_h[s]])
                    yield

                def ch_front2(t):
                    s = t % 2
                    for j in range(8):
                        P.op("pe", lambda e, j=j: e.transpose(out=pT[s][:, j, :], in_=h[s][:, j * 128:(j + 1) * 128],
                                                              identity=ident[:]), reads=[r_h[s]], writes=[r_pT[s]])
                    yield
                    P.op("act", lambda e: e.copy(out=hT[s][:], in_=pT[s][:]), reads=[r_pT[s]], writes=[r_hT[s]])
                    yield
                    for c3 in range(3):
                        for kc in range(8):
                            P.op("pe", lambda e, c3=c3, kc=kc: e.matmul(
                                out=pz[c3][:], lhsT=hT[s][:, kc, :], rhs=w_in[:, kc, c3 * 512:(c3 + 1) * 512],
                                start=(kc == 0), stop=(kc == 7)), reads=[r_hT[s], r_w2[c3]], writes=[r_pz[c3]])
                        yield

                def ch_gelu(t):
                    s = t % 2
                    P.op("act", lambda e: e.activation(out=ug[s][:], in_=pz[0][:, 256:512], func=AF.Gelu_apprx_tanh),
                         reads=[r_pz[0]], writes=[r_ug[s]])
                    P.op("act", lambda e: e.activation(out=vg[s][:], in_=pz[1][:, 0:256], func=AF.Gelu_apprx_tanh),
                         reads=[r_pz[1]], writes=[r_vg[s]])
                    yield

                def ch_misc(t):
                    s = t % 2
                    P.op("act", lambda e: e.copy(out=zf[s][:], in_=pz[0][:, 0:256]), reads=[r_pz[0]], writes=[r_zf[s]])
                    P.dma("sp", xf_d[t], zf[s][:], reads=[r_zf[s]], chan=czf[s])
                    yield
                    P.op("act", lambda e: e.copy(out=V[:, t, :, 0:64],
                                                 in_=pz[2][:, 384:512].rearrange("p (k d) -> p k d", k=2)),
                         reads=[r_pz[2]], writes=[r_V])
                    yield

                def ch_sgu(t):
                    s = t % 2
                    P.op("dve", lambda e: e.bn_stats(out=bst[s][:], in_=vg[s][:]), reads=[r_vg[s]], writes=[r_bst[s]])
                    yield
                    P.op("dve", lambda e: e.bn_aggr(out=mv[s][:], in_=bst[s][:]), reads=[r_bst[s]], writes=[r_mv[s]])
                    yield
                    P.op("act", lambda e: e.activation(out=rsv[s][:], in_=mv[s][:, 1:2], func=AF.Ln, bias=epsb[:, 0:1]),
                         reads=[r_mv[s]], writes=[r_rsv[s]])
                    yield
                    P.op("act", lambda e: e.activation(out=rsv[s][:], in_=rsv[s][:], func=AF.Exp, scale=-0.5),
                         reads=[r_rsv[s]], writes=[r_rsv[s]])
                    yield
                    P.op("dve", lambda e: e.tensor_scalar(out=vn[s][:], in0=vg[s][:], scalar1=mv[s][:, 0:1], scalar2=rsv[s][:, 0:1],
                                                          op0=ALU.subtract, op1=ALU.mult),
                         reads=[r_vg[s], r_mv[s], r_rsv[s]], writes=[r_vn[s]])
                    yield
                    P.op("dve", lambda e: e.tensor_tensor(out=vnb[s][:], in0=vn[s][:], in1=gv[:], op=ALU.mult),
                         reads=[r_vn[s], r_gv], writes=[r_vnb[s]])
                    yield
                    for hh in range(4):
                        P.op("pe", lambda e, hh=hh: e.matmul(out=psv[s][:, hh * 64:(hh + 1) * 64], lhsT=wsT[:, hh, :],
                                                             rhs=vnb[s][:, hh * 64:(hh + 1) * 64], start=True, stop=True),
                             reads=[r_wsT, r_vnb[s]], writes=[r_psv[s]])
                    yield
                    P.op("dve", lambda e: e.tensor_tensor(out=ys[s][:].rearrange("p (h c) -> p h c", h=4),
                                                          in0=psv[s][:, 0:256].rearrange("p (h c) -> p h c", h=4),
                                                          in1=bcast(bpp[:], 2, 64), op=ALU.add),
                         reads=[r_psv[s], r_bpp], writes=[r_ys[s]])
                    yield
                    P.op("dve", lambda e: e.tensor_tensor(out=ys[s][:], in0=ys[s][:], in1=ug[s][:], op=ALU.mult),
                         reads=[r_ys[s], r_ug[s]], writes=[r_ys[s]])
                    yield

                def ch_sgu_b(t):
                    s = t % 2
                    P.op("act", lambda e: e.activation(out=junk2[:], in_=ys[s][:], func=AF.Square, accum_out=ss2[s][:]),
                         reads=[r_ys[s]], writes=[r_junk2, r_ss2[s]])
                    yield
                    P.op("act", lambda e: e.activation(out=rs2[s][:], in_=ss2[s][:], func=AF.Ln, scale=1.0 / 256, bias=epsb[:, 0:1]),
                         reads=[r_ss2[s]], writes=[r_rs2[s]])
                    yield
                    P.op("act", lambda e: e.activation(out=rs2[s][:], in_=rs2[s][:], func=AF.Exp, scale=-0.5),
                         reads=[r_rs2[s]], writes=[r_rs2[s]])
                    yield
                    P.op("dve", lambda e: e.scalar_tensor_tensor(out=ysb[s][:], in0=ys[s][:], scalar=rs2[s][:, 0:1],
                                                                 in1=gms[:], op0=ALU.mult, op1=ALU.mult),
                         reads=[r_ys[s], r_rs2[s], r_gms], writes=[r_ysb[s]])
                    P.dma("sp", y_d[t * 128:(t + 1) * 128, 256:512], ysb[s][:], reads=[r_ysb[s]], chan=cys[s])
                    yield

                def ch_qk(t):
                    s = t % 2
                    P.op("act", lambda e: e.copy(out=qk[s][:, 0:256], in_=pz[1][:, 256:512]), reads=[r_pz[1]], writes=[r_qk[s]])
                    yield
                    P.op("act", lambda e: e.copy(out=qk[s][:, 256:640], in_=pz[2][:, 0:384]), reads=[r_pz[2]], writes=[r_qk[s]])
                    yield
                    P.op("dve", lambda e: e.tensor_tensor(out=qa[s][:], in0=qk[s][:], in1=qk[s][:], op=ALU.mult),
                         reads=[r_qk[s]], writes=[r_qa[s]])
                    yield
                    P.op("dve", lambda e: e.tensor_reduce(out=qss[s][:], in_=qk3(qa[s]), axis=AX.X, op=ALU.add),
                         reads=[r_qa[s]], writes=[r_qss[s]])
                    yield
                    P.op("act", lambda e: e.activation(out=qrs[s][:], in_=qss[s][:], func=AF.Ln, scale=1.0 / 64, bias=epsb[:, 0:1]),
                         reads=[r_qss[s]], writes=[r_qrs[s]])
                    yield
                    P.op("act", lambda e: e.activation(out=qrs[s][:], in_=qrs[s][:], func=AF.Exp, scale=-0.5),
                         reads=[r_qrs[s]], writes=[r_qrs[s]])
                    yield
                    P.op("dve", lambda e: e.tensor_tensor(out=qk3(qa[s]), in0=qk3(qk[s]), in1=bcast(qrs[s][:], 2, 64), op=ALU.mult),
                         reads=[r_qk[s], r_qrs[s]], writes=[r_qa[s]])
                    yield
                    P.op("dve", lambda e: e.tensor_tensor(out=qa[s][:], in0=qa[s][:], in1=gtab[:].rearrange("p h d -> p (h d)"),
                                                           op=ALU.mult), reads=[r_qa[s], r_gtab], writes=[r_qa[s]])
                    yield
                    P.op("dve", lambda e: e.tensor_tensor(out=qk3(qb[s]), in0=qk3(qa[s]), in1=bcast(cos[:, t, :], 1, 10),
                                                          op=ALU.mult), reads=[r_qa[s], r_cos], writes=[r_qb[s]])
                    yield

                    def sw(e, xo, xi):
                        o_ = qk5(qk[s])[:, :, xo, :].rearrange("p (h f) d -> p h f d", h=10)
                        i_ = qk5(qa[s])[:, :, xi, :].rearrange("p (h f) d -> p h f d", h=10)
                        s_ = bcast(sin4(t)[:, :, xo, :], 1, 10)
                        return e.tensor_tensor(out=o_, in0=i_, in1=s_, op=ALU.mult)
                    P.op("dve", lambda e: sw(e, 0, 1), reads=[r_qa[s], r_sins, r_qk[s]], writes=[r_qk[s]])
                    yield
                    P.op("dve", lambda e: sw(e, 1, 0), reads=[r_qa[s], r_sins, r_qk[s]], writes=[r_qk[s]])
                    yield
                    P.op("dve", lambda e: e.tensor_tensor(
                        out=qkr[s][:, 0:512].rearrange("p (j e d) -> p j e d", j=4, e=2),
                        in0=qb[s][:, 0:512].rearrange("p (e j d) -> p j e d", e=2, j=4),
                        in1=qk[s][:, 0:512].rearrange("p (e j d) -> p j e d", e=2, j=4), op=ALU.add),
                        reads=[r_qb[s], r_qk[s]], writes=[r_qkr[s]])
                    yield
                    P.op("dve", lambda e: e.tensor_tensor(out=qkr[s][:, 512:640], in0=qb[s][:, 512:640],
                                                          in1=qk[s][:, 512:640], op=ALU.add),
                         reads=[r_qb[s], r_qk[s], r_qkr[s]], writes=[r_qkr[s]])
                    yield

                def ch_tail(t):
                    s = t % 2
                    for j in range(5):
                        P.op("pe", lambda e, j=j: e.transpose(out=pT2[:, j, :], in_=qkr[s][:, j * 128:(j + 1) * 128],
                                                              identity=ident[:]), reads=[r_qkr[s]], writes=[r_pT2])
                    yield
                    P.op("act", lambda e: e.copy(out=qT[:, :, t * 128:(t + 1) * 128], in_=pT2[:, 0:4, :]),
                         reads=[r_pT2], writes=[r_qT])
                    yield
                    P.op("dve", lambda e: e.tensor_copy(out=kT[:, t * 128:(t + 1) * 128], in_=pT2[:, 4, :]),
                         reads=[r_pT2], writes=[r_kT])
                    yield

                def rr(gens):
                    gens = list(gens)
                    while gens:
                        for g in list(gens):
                            try:
                                next(g)
                            except StopIteration:
                                gens.remove(g)

                rr([ch_load(0)])
                rr([ch_load(1), ch_front(0)])
                rr([ch_load(2), ch_front(1), ch_front2(0)])
                for t in range(NT + 1):
                    if t == NT:
                        rr([ch_tail(t - 1), ch_sgu_b(t - 1)])
                        break
                    rr([ch_gelu(t)])
                    gens = [ch_misc(t), ch_sgu(t), ch_qk(t)]
                    if t >= 1:
                        gens.append(delayed(2, ch_tail(t - 1)))
                        gens.append(delayed(2, ch_sgu_b(t - 1)))
                    if t + 1 < NT:
                        gens.insert(0, ch_front2(t + 1))
                    if t + 2 < NT:
                        gens.insert(0, ch_front(t + 2))
                    if t + 3 < NT:
                        gens.insert(0, ch_load(t + 3))
                    rr(gens)
                P.run_block("A")

        def phase_F(l):
            with contextlib.ExitStack() as st:
                f1 = sb("f1", [32, 64], BF16, st)
                f2 = sb("f2", [128, 32, 3, 128], BF16, st)
                f3 = sb("f3", [128, 2, 128], BF16, st)
                xin = [sb("xin%d" % i, [32, 4096], BF16, st) for i in range(3)]
                t1s = [sb("t1s%d" % i, [64, 4096], BF16, st) for i in range(2)]
                T2 = sb("T2", [128, 64, 256], BF16, st)
                GT = sb("GT", [128, 2, 2, S], BF16, st)
                gmf = sb("gmf", [128, 256], F32, st)
                junk = sb("junkF", [128, 256], BF16, st)
                ssF = [sb("ssF%d" % i, [128, 1], F32, st) for i in range(2)]
                rsF = [sb("rsF%d" % i, [128, 1], F32, st) for i in range(2)]
                yfb = [sb("yfb%d" % i, [128, 256], BF16, st) for i in range(2)]
                p1 = [ps("p1_%d" % i, [64, 512], F32, st) for i in range(2)]
                p2 = [ps("p2_%d" % i, [128, 512], F32, st) for i in range(2)]
                p3 = [ps("p3_%d" % i, [128, 512], F32, st) for i in range(2)]

                P.begin()
                R = lambda n="": Res(n)
                cc = P.chans(4, "cstF")
                r_f1, r_f2, r_f3, r_gmf = R(), R(), R(), R()
                P.dma("sp", f1[:], T["f1"], writes=[r_f1], chan=cc[0])
                P.dma("sp", f2[:], T["f2"], writes=[r_f2], chan=cc[1])
                P.dma("sp", f3[:], T["f3"], writes=[r_f3], chan=cc[2])
                P.dma("sp", gmf[:], pbcast(T["g_mix"][l, 0:256]), writes=[r_gmf], chan=cc[3])
                xflat = xf_d.rearrange("t p c -> t (p c)")
                tflat = t1_d.rearrange("r p c -> r (p c)")
                r_xin = [R(), R(), R()]
                r_t1s = [R(), R()]
                r_t1sd = [R(), R()]
                r_p1 = [RP(), RP()]
                r_p2 = [RP(), RP()]
                r_p3 = [RP(), RP()]
                r_t1d = [R() for _ in range(8)]
                cxin = P.chans(3, "xin")
                ct1 = P.chans(2, "t1s")
                nev = 0
                cT2 = P.chans(8, "T2", fresh=True)
                r_T2 = [R() for _ in range(8)]
                t1v = t1_d.rearrange("r p c -> p r c")
                NX = 3

                def ld(ci):
                    s3 = ci % NX
                    P.dma("sp", xin[s3][:], xflat[:, ci * 4096:(ci + 1) * 4096], writes=[r_xin[s3]], chan=cxin[s3])

                ld(0)
                ld(1)
                for ci in range(8):
                    s = ci % 2
                    s3 = ci % NX
                    if ci + 2 < 8:
                        ld(ci + 2)
                    for sub in range(8):
                        b = sub % 2
                        P.op("pe", lambda e, s3=s3, sub=sub, b=b: e.matmul(
                            out=p1[b][:], lhsT=f1[:], rhs=xin[s3][:, sub * 512:(sub + 1) * 512], start=True, stop=True),
                            reads=[r_f1, r_xin[s3]], writes=[r_p1[b]])
                        eng = "act" if nev % 2 == 0 else "dve"
                        nev += 1
                        if eng == "act":
                            P.op("act", lambda e, s=s, sub=sub, b=b: e.copy(out=t1s[s][:, sub * 512:(sub + 1) * 512], in_=p1[b][:]),
                                 reads=[r_p1[b]], writes=[r_t1s[s]])
                        else:
                            P.op("dve", lambda e, s=s, sub=sub, b=b: e.tensor_copy(out=t1s[s][:, sub * 512:(sub + 1) * 512], in_=p1[b][:]),
                                 reads=[r_p1[b]], writes=[r_t1sd[s]])
                    P.dma("sp", tflat[:, ci * 4096:(ci + 1) * 4096], t1s[s][:], reads=[r_t1s[s], r_t1sd[s]], writes=[r_t1d[ci]], chan=ct1[s])
                    P.dma("pool", T2[16 * ci:16 * ci + 16, :, :], t1v[16 * ci:16 * ci + 16, :, :], reads=[r_t1d[ci]],
                          writes=[r_T2[ci]], chan=cT2[ci])
                r_GT = R()
                r_GTd = R()
                GTv = [GT[:, j, :, :].rearrange("p r (kt kp) -> p r kt kp", kt=32) for j in range(2)]
                r_GTk = [[R(), R()] for _ in range(32)]
                r_ss, r_rs = [R(), R()], [R(), R()]
                r_junk = R()
                r_yfb = [R(), R()]
                cyf = P.chans(2, "yf")

                def s3(t):
                    b = t % 2
                    for j in range(2):
                        for ri in range(2):
                            P.op("pe", lambda e, t=t, j=j, ri=ri, b=b: e.matmul(
                                out=p3[b][:, j * 128:(j + 1) * 128], lhsT=GTv[j][:, ri, t, :],
                                rhs=f3[:, ri, :], start=(ri == 0), stop=(ri == 1)),
                                reads=[r_GTk[t][j], r_f3], writes=[r_p3[b]])
                    P.op("act", lambda e, b=b: e.activation(out=junk[:], in_=p3[b][:, 0:256], func=AF.Square, accum_out=ssF[b][:]),
                         reads=[r_p3[b]], writes=[r_junk, r_ss[b]])
                    P.op("act", lambda e, b=b: e.activation(out=rsF[b][:], in_=ssF[b][:], func=AF.Ln, scale=1.0 / 256, bias=epsb[:, 0:1]),
                         reads=[r_ss[b]], writes=[r_rs[b]])
                    P.op("act", lambda e, b=b: e.activation(out=rsF[b][:], in_=rsF[b][:], func=AF.Exp, scale=-0.5),
                         reads=[r_rs[b]], writes=[r_rs[b]])
                    P.op("dve", lambda e, b=b: e.scalar_tensor_tensor(out=yfb[b][:], in0=p3[b][:, 0:256], scalar=rsF[b][:, 0:1],
                                                                      in1=gmf[:], op0=ALU.mult, op1=ALU.mult),
                         reads=[r_p3[b], r_rs[b], r_gmf], writes=[r_yfb[b]])
                    P.dma("sp", y_d.rearrange("(kp kt) c -> kt kp c", kt=32)[t, :, 0:256], yfb[b][:], reads=[r_yfb[b]], chan=cyf[b])

                idx = 0
                for kt in range(32):
                    for j in range(2):
                        b = (idx // 2) % 2
                        hf = idx % 2
                        rr = r_T2
                        P.op("pe", lambda e, kt=kt, j=j, b=b, hf=hf: e.matmul(
                            out=p2[b][:, hf * 256:(hf + 1) * 256], lhsT=T2[:, kt, j * 128:(j + 1) * 128],
                            rhs=f2[:, kt, 1:3, :].rearrange("p a k -> p (a k)"), start=True, stop=False),
                            reads=rr + [r_f2], writes=[r_p2[b]])
                        P.op("pe", lambda e, kt=kt, j=j, b=b, hf=hf: e.matmul(
                            out=p2[b][:, hf * 256:(hf + 1) * 256], lhsT=T2[:, 32 + kt, j * 128:(j + 1) * 128],
                            rhs=f2[:, kt, 0:2, :].rearrange("p a k -> p (a k)"), start=False, stop=True),
                            reads=rr + [r_f2], writes=[r_p2[b]])
                        src_ = lambda b=b, hf=hf: p2[b][:, hf * 256:(hf + 1) * 256].rearrange("p (r k) -> p r k", r=2)
                        dst_ = lambda kt=kt, j=j: GTv[j][:, :, kt, :]
                        if idx % 2 == 0:
                            P.op("act", lambda e, src_=src_, dst_=dst_: e.copy(out=dst_(), in_=src_()),
                                 reads=[r_p2[b]], writes=[r_GTk[kt][j]])
                        else:
                            P.op("dve", lambda e, src_=src_, dst_=dst_: e.tensor_copy(out=dst_(), in_=src_()),
                                 reads=[r_p2[b]], writes=[r_GTk[kt][j]])
                        idx += 1
                    if kt >= 2:
                        s3(kt - 2)
                s3(30)
                s3(31)
                P.run_block("F")

        def phase_B(l, qT, kT, V, W):
            with contextlib.ExitStack() as st:
                NPT = 4
                NS = 3
                pt = [sb("pt%d" % i, [128, 1024], BF16, st) for i in range(NPT)]
                oT = [sb("oT%d" % i, [128, 512], F32, st) for i in range(2)]
                ya = [sb("ya%d" % i, [128, 4, 512], F32, st) for i in range(2)]
                rec = [sb("rec%d" % i, [128, 4], F32, st) for i in range(2)]
                junk = [sb("junkB%d" % i, [128, 512], F32, st) for i in range(2)]
                ssB = sb("ssB", [128, 4], F32, st)
                rsB = sb("rsB", [128, 4], F32, st)
                gma = sb("gma", [128, 512], F32, st)
                yab = [sb("yab%d" % i, [128, 512], BF16, st) for i in range(2)]
                psS = [ps("psS%d" % i, [128, 1024], F32, st) for i in range(NS)]
                psO = [ps("psO%d" % i, [128, 512], F32, st) for i in range(2)]

                P.begin()
                R = lambda n="": Res(n)
                cg = P.chan("gma")
                r_gma = R()
                P.dma("sp", gma[:], pbcast(T["g_mix"][l, 512:1024]), writes=[r_gma], chan=cg)
                cwp = P.chans(8, "wpre", fresh=True)
                for k2 in range(4):
                    P.dma("pool", W["g"][:, 2 * k2:2 * k2 + 2, :],
                          T["w_gate"][l, k2 * 256:(k2 + 1) * 256, :].rearrange("(k p) n -> p k n", p=128), chan=cwp[2 * k2])
                    P.dma("pool", W["u"][:, 2 * k2:2 * k2 + 2, :],
                          T["w_up"][l, k2 * 256:(k2 + 1) * 256, :].rearrange("(k p) n -> p k n", p=128), chan=cwp[2 * k2 + 1])
                cwo = P.chans(4, "w_o", fresh=True)
                for k2 in range(4):
                    P.dma("pool", W["o"][:, 2 * k2:2 * k2 + 2, :],
                          T["w_out"][l, k2 * 256:(k2 + 1) * 256, :].rearrange("(k p) n -> p k n", p=128), chan=cwo[k2])
                r_pt = [R() for _ in range(NPT)]
                r_psS = [RP() for _ in range(NS)]
                r_psO = [RP(), RP()]
                r_oT = [R(), R()]
                r_ya = [R(), R()]
                r_yab = [R(), R()]
                r_rec = [R(), R()]
                r_ss = R()
                r_rs = R()
                r_junk = [R(), R()]
                cya = P.chans(2, "ya")

                groups = [(qb, j, kt) for qb in range(8) for j in range(4) for kt in range(NT)]
                N = len(groups)
                pending_fin = []
                ring = [0]

                def emit_S(i):
                    qb, j, kt = groups[i]
                    sbi = ring[0] % NS
                    ring[0] += 1
                    for e_ in range(2):
                        P.op("pe", lambda e, qb=qb, j=j, e_=e_, kt=kt, sbi=sbi: e.matmul(
                            out=psS[sbi][:, e_ * 512:(e_ + 1) * 512], lhsT=kT[64 * e_:64 * e_ + 64, kt * 128:(kt + 1) * 128],
                            rhs=qT[64 * e_:64 * e_ + 64, j, qb * 512:(qb + 1) * 512], start=True, stop=True),
                            writes=[r_psS[sbi]])
                    pi = i % NPT
                    P.op("act", lambda e, sbi=sbi, pi=pi: e.activation(out=pt[pi][:], in_=psS[sbi][:], func=AF.Exp, scale=0.125),
                         reads=[r_psS[sbi]], writes=[r_pt[pi]])

                def emit_PV(i):
                    qb, j, kt = groups[i]
                    pi = i % NPT
                    for e_ in range(2):
                        P.op("pe", lambda e, e_=e_, kt=kt, pi=pi: e.matmul(
                            out=psO[e_][:, :], lhsT=V[:, kt, e_, :], rhs=pt[pi][:, e_ * 512:(e_ + 1) * 512],
                            start=(kt == 0), stop=(kt == NT - 1)),
                            reads=[r_pt[pi]], writes=[r_psO[e_]])
                    if kt == NT - 1:
                        yb = qb % 2
                        for e_ in range(2):
                            P.op("dve", lambda e, e_=e_: e.tensor_copy(out=oT[e_][:], in_=psO[e_][:, :]),
                                 reads=[r_psO[e_]], writes=[r_oT[e_]])

                        def fin(j=j, yb=yb, qb=qb, last=(j == 3)):
                            sbi = ring[0] % NS
                            ring[0] += 1
                            pv = psS[sbi][:].rearrange("p (b c) -> p b c", b=8)
                            for e_ in range(2):
                                for i4 in range(4):
                                    P.op("pe", lambda e, i4=i4, e_=e_, pv=pv: e.transpose(
                                        out=pv[:, e_ * 4 + i4, :], in_=oT[e_][:, i4 * 128:(i4 + 1) * 128], identity=identf[:]),
                                        reads=[r_oT[e_]], writes=[r_psS[sbi]])
                            for e_ in range(2):
                                h = 4 * e_ + j
                                P.op("dve", lambda e, e_=e_, pv=pv: e.reciprocal(out=rec[e_][:], in_=pv[:, e_ * 4:e_ * 4 + 4, 64]),
                                     reads=[r_psS[sbi]], writes=[r_rec[e_]])
                                P.op("dve", lambda e, e_=e_, h=h, yb=yb, pv=pv: e.tensor_tensor(
                                    out=ya[yb][:, :, h * 64:(h + 1) * 64], in0=pv[:, e_ * 4:e_ * 4 + 4, 0:64], in1=bcast(rec[e_][:], 2, 64), op=ALU.mult),
                                    reads=[r_psS[sbi], r_rec[e_]], writes=[r_ya[yb]])
                            if last:
                                pending_fin.append((i + 6, lambda yb=yb, qb=qb: fin2a(yb, qb)))
                                pending_fin.append((i + 10, lambda yb=yb, qb=qb: fin2b(yb, qb)))
                                pending_fin.sort(key=lambda x: x[0])

                        def fin2a(yb, qb):
                            for i4 in range(4):
                                jb = i4 % 2
                                P.op("pool", lambda e, yb=yb, i4=i4, jb=jb: e.tensor_tensor(out=junk[jb][:], in0=ya[yb][:, i4, :], in1=ya[yb][:, i4, :], op=ALU.mult),
                                     reads=[r_ya[yb]], writes=[r_junk[jb]])
                                P.op("dve", lambda e, i4=i4, jb=jb: e.tensor_reduce(out=ssB[:, i4:i4 + 1], in_=junk[jb][:], axis=AX.X, op=ALU.add),
                                     reads=[r_junk[jb]], writes=[r_ss])

                        def fin2b(yb, qb):
                            P.op("act", lambda e: e.activation(out=rsB[:], in_=ssB[:], func=AF.Ln, scale=1.0 / 512, bias=epsb[:, 0:1]),
                                 reads=[r_ss], writes=[r_rs])
                            P.op("act", lambda e: e.activation(out=rsB[:], in_=rsB[:], func=AF.Exp, scale=-0.5),
                                 reads=[r_rs], writes=[r_rs])
                            for i4 in range(4):
                                t = qb * 4 + i4
                                s2 = i4 % 2
                                P.op("dve", lambda e, yb=yb, i4=i4, s2=s2: e.scalar_tensor_tensor(
                                    out=yab[s2][:], in0=ya[yb][:, i4, :], scalar=rsB[:, i4:i4 + 1], in1=gma[:], op0=ALU.mult, op1=ALU.mult),
                                    reads=[r_ya[yb], r_rs, r_gma], writes=[r_yab[s2]])
                                P.dma("sp", y_d[t * 128:(t + 1) * 128, 512:1024], yab[s2][:], reads=[r_yab[s2]], chan=cya[s2])
                        pending_fin.append((i + 4, fin))
                        pending_fin.sort(key=lambda x: x[0])

                SK = 2
                for i in range(N + SK):
                    if i < N:
                        emit_S(i)
                    if i >= SK:
                        emit_PV(i - SK)
                    while pending_fin and pending_fin[0][0] <= i - SK:
                        pending_fin.pop(0)[1]()
                while pending_fin:
                    pending_fin.pop(0)[1]()
                P.run_block("B")

        def rr(gens):
            gens = list(gens)
            while gens:
                for g in list(gens):
                    try:
                        next(g)
                    except StopIteration:
                        gens.remove(g)

        def phase_C(l, src, W):
            w_o = W["o"]
            with contextlib.ExitStack() as st:
                gpm = sb("gpm", [128, D], F32, st)
                yt = [sb("yt%d" % i, [128, D], BF16, st) for i in range(3)]
                yT = [sb("yT%d" % i, [128, 8, 128], BF16, st) for i in range(2)]
                xs = [sb("xsC%d" % i, [128, D], F32, st) for i in range(3)]
                tmp = [sb("tmpC%d" % i, [128, D], F32, st) for i in range(2)]
                junk = sb("junkC", [128, D], BF16, st)
                ssC = [sb("ssC%d" % i, [128, 1], F32, st) for i in range(2)]
                rsC = [sb("rsC%d" % i, [128, 1], F32, st) for i in range(2)]
                pT = [ps("pTC%d" % i, [128, 8, 128], BF16, st) for i in range(2)]
                pm = [ps("pm%d" % i, [128, D], F32, st) for i in range(2)]

                P.begin()
                R = lambda n="": Res(n)
                r_w = [R() for _ in range(8)]
                cwd = P.chans(NFF // 2, "w_dpre", fresh=True)
                for f2 in range(NFF // 2):
                    P.dma("pool", W["d"][:, 2 * f2:2 * f2 + 2, :],
                          T["w_down"][l, f2 * 256:(f2 + 1) * 256, :].rearrange("(k p) n -> p k n", p=128), chan=cwd[f2])
                cgp = P.chan("gpm")
                r_gpm = R()
                P.dma("sp", gpm[:], pbcast(T["g_post_mix"][l]), writes=[r_gpm], chan=cgp)
                r_yT, r_yTd, r_tmp, r_ss, r_rs = [[R(), R()] for _ in range(5)]
                r_yt = [R() for _ in range(3)]
                r_xs = [R() for _ in range(3)]
                r_pT, r_pm = [[RP(), RP()] for _ in range(2)]
                r_junk = R()
                cy = P.chans(3, "ytl")
                cx = P.chans(3, "xsl")
                cxo = P.chans(3, "xso")

                def c_load(t):
                    y3 = t % 3
                    rows = slice(t * 128, (t + 1) * 128)
                    P.dma("sp", yt[y3][:], y_d[rows, :], writes=[r_yt[y3]], chan=cy[y3])
                    yield

                def c_loadx(t):
                    x3 = t % 3
                    rows = slice(t * 128, (t + 1) * 128)
                    P.dma("sp", xs[x3][:], src[rows, :], writes=[r_xs[x3]], chan=cx[x3])
                    yield

                def c_fa(t):
                    s = t % 2
                    y3 = t % 3
                    for j in range(8):
                        P.op("pe", lambda e, j=j: e.transpose(out=pT[s][:, j, :], in_=yt[y3][:, j * 128:(j + 1) * 128], identity=ident[:]),
                             reads=[r_yt[y3]], writes=[r_pT[s]])
                    yield
                    P.op("act", lambda e: e.copy(out=yT[s][:, 0:4, :], in_=pT[s][:, 0:4, :]), reads=[r_pT[s]], writes=[r_yT[s]])
                    yield
                    P.op("dve", lambda e: e.tensor_copy(out=yT[s][:, 4:8, :], in_=pT[s][:, 4:8, :]), reads=[r_pT[s]], writes=[r_yTd[s]])
                    yield

                def c_fb(t):
                    s = t % 2
                    for n in range(2):
                        for kc in range(8):
                            P.op("pe", lambda e, n=n, kc=kc: e.matmul(
                                out=pm[s][:, n * 512:(n + 1) * 512], lhsT=yT[s][:, kc, :], rhs=w_o[:, kc, n * 512:(n + 1) * 512],
                                start=(kc == 0), stop=(kc == 7)), reads=[r_yT[s], r_yTd[s], r_w[kc]], writes=[r_pm[s]])
                        yield

                def c_back(t):
                    s = t % 2
                    x3 = t % 3
                    rows = slice(t * 128, (t + 1) * 128)
                    P.op("act", lambda e: e.activation(out=junk[:], in_=pm[s][:], func=AF.Square, accum_out=ssC[s][:]),
                         reads=[r_pm[s]], writes=[r_junk, r_ss[s]])
                    yield
                    P.op("act", lambda e: e.activation(out=rsC[s][:], in_=ssC[s][:], func=AF.Ln, scale=1.0 / D, bias=epsb[:, 0:1]),
                         reads=[r_ss[s]], writes=[r_rs[s]])
                    yield
                    P.op("act", lambda e: e.activation(out=rsC[s][:], in_=rsC[s][:], func=AF.Exp, scale=-0.5), reads=[r_rs[s]], writes=[r_rs[s]])
                    yield
                    P.op("dve", lambda e: e.scalar_tensor_tensor(out=tmp[s][:], in0=pm[s][:], scalar=rsC[s][:, 0:1], in1=gpm[:],
                                                                 op0=ALU.mult, op1=ALU.mult),
                         reads=[r_pm[s], r_rs[s], r_gpm], writes=[r_tmp[s]])
                    yield
                    P.op("pool", lambda e: e.tensor_tensor(out=xs[x3][:], in0=xs[x3][:], in1=tmp[s][:], op=ALU.add),
                         reads=[r_xs[x3], r_tmp[s]], writes=[r_xs[x3]])
                    yield
                    P.dma("sp", out[rows, :], xs[x3][:], reads=[r_xs[x3]], chan=cxo[x3])
                    yield

                rr([c_load(0), c_load(1), c_load(2), c_loadx(0), c_loadx(1)])
                rr([c_fa(0)])
                rr([c_fa(1), c_fb(0)])
                for t in range(NT):
                    gens = [c_back(t)]
                    if t + 1 < NT:
                        gens.insert(0, c_fb(t + 1))
                    if t + 2 < NT:
                        gens.insert(0, c_fa(t + 2))
                        gens.insert(0, c_loadx(t + 2))
                    if t + 3 < NT:
                        gens.insert(0, c_load(t + 3))
                    rr(gens)
                P.run_block("C")

        def phase_D(l, W):
            G = 2
            NG = NT // G
            GW = 128 * G
            w_g, w_u, w_d = W["g"], W["u"], W["d"]
            with contextlib.ExitStack() as st:
                gpf = sb("gpf", [128, D], F32, st)
                gpo = sb("gpo", [128, D], F32, st)
                xs = [sb("xsD%d" % i, [128, G, D], F32, st) for i in range(2)]
                h2 = [sb("h2_%d" % i, [128, D], BF16, st) for i in range(2)]
                h2T = sb("h2T", [128, 8, GW], BF16, st)
                sg = [sb("sg%d" % i, [128, GW], F32, st) for i in range(2)]
                aT = sb("aT", [128, NFF, GW], BF16, st)
                tmp = sb("tmpD", [128, D], F32, st)
                junk = sb("junkD", [128, D], BF16, st)
                ssD = sb("ssD", [128, 2], F32, st)
                rsD = sb("rsD", [128, 2], F32, st)
                ssE = [sb("ssE%d" % i, [128, 1], F32, st) for i in range(2)]
                rsE = [sb("rsE%d" % i, [128, 1], F32, st) for i in range(2)]
                pT = [ps("pTD%d" % i, [128, 8, 128], BF16, st) for i in range(2)]
                pg = [ps("pg%d" % i, [128, 512], F32, st) for i in range(2)]
                pu = [ps("pu%d" % i, [128, 512], F32, st) for i in range(2)]
                pf = ps("pf", [128, D], F32, st)

                P.begin()
                R = lambda n="": Res(n)
                cgg = P.chans(2, "gD")
                r_gpf, r_gpo = R(), R()
                P.dma("sp", gpf[:], pbcast(T["g_pre_ffn"][l]), writes=[r_gpf], chan=cgg[0])
                P.dma("sp", gpo[:], pbcast(T["g_post_ffn"][l]), writes=[r_gpo], chan=cgg[1])
                r_xs = [[R() for _ in range(G)] for _ in range(2)]
                r_sg, r_h2, r_ssE, r_rsE = [[R(), R()] for _ in range(4)]
                r_ssD, r_rsD = R(), R()
                r_pg, r_pu = [[RP(), RP()] for _ in range(2)]
                r_h2T, r_aT, r_tmp, r_junk = [R() for _ in range(4)]
                r_pT, r_pfh = [RP(), RP()], [RP(), RP()]
                cx = [P.chans(G, "xsD%d" % i) for i in range(2)]
                cxo = [P.chans(G, "xoD%d" % i) for i in range(2)]

                def d_load(gi):
                    s = gi % 2
                    for i in range(G):
                        t = gi * G + i
                        P.dma("sp", xs[s][:, i, :], out[t * 128:(t + 1) * 128, :], writes=[r_xs[s][i]], chan=cx[s][i])
                        yield

                def d_front_a(gi):
                    s = gi % 2
                    for i in range(G):
                        P.op("act", lambda e, i=i: e.activation(out=junk[:], in_=xs[s][:, i, :], func=AF.Square, accum_out=ssD[:, i:i + 1]),
                             reads=[r_xs[s][i]], writes=[r_junk, r_ssD])
                        yield
                    P.op("act", lambda e: e.activation(out=rsD[:], in_=ssD[:], func=AF.Ln, scale=1.0 / D, bias=epsb[:, 0:1]),
                         reads=[r_ssD], writes=[r_rsD])
                    P.op("act", lambda e: e.activation(out=rsD[:], in_=rsD[:], func=AF.Exp, scale=-0.5),
                         reads=[r_rsD], writes=[r_rsD])
                    yield
                    for i in range(G):
                        P.op("dve", lambda e, i=i: e.scalar_tensor_tensor(out=h2[i][:], in0=xs[s][:, i, :], scalar=rsD[:, i:i + 1], in1=gpf[:],
                                                                          op0=ALU.mult, op1=ALU.mult),
                             reads=[r_xs[s][i], r_rsD, r_gpf], writes=[r_h2[i]])
                        yield

                def d_front_b(gi):
                    for i in range(G):
                        for j in range(8):
                            P.op("pe", lambda e, i=i, j=j: e.transpose(out=pT[i][:, j, :], in_=h2[i][:, j * 128:(j + 1) * 128], identity=ident[:]),
                                 reads=[r_h2[i]], writes=[r_pT[i]])
                        yield
                    for i in range(G):
                        P.op("act", lambda e, i=i: e.copy(out=h2T[:, :, i * 128:(i + 1) * 128], in_=pT[i][:]),
                             reads=[r_pT[i]], writes=[r_h2T])
                        yield

                fi = [0]

                def d_gateup(gi):
                    for f in range(NFF):
                        b = fi[0] % 2
                        fi[0] += 1
                        for kc in range(8):
                            P.op("pe", lambda e, f=f, kc=kc, b=b: e.matmul(
                                out=pg[b][:, 0:GW], lhsT=w_g[:, kc, f * 128:(f + 1) * 128], rhs=h2T[:, kc, :],
                                start=(kc == 0), stop=(kc == 7)), reads=[r_h2T], writes=[r_pg[b]])
                        for kc in range(8):
                            P.op("pe", lambda e, f=f, kc=kc, b=b: e.matmul(
                                out=pu[b][:, 0:GW], lhsT=w_u[:, kc, f * 128:(f + 1) * 128], rhs=h2T[:, kc, :],
                                start=(kc == 0), stop=(kc == 7)), reads=[r_h2T], writes=[r_pu[b]])
                        P.op("act", lambda e, b=b: e.activation(out=sg[b][:], in_=pg[b][:, 0:GW], func=AF.Silu),
                             reads=[r_pg[b]], writes=[r_sg[b]])
                        P.op("dve", lambda e, f=f, b=b: e.tensor_tensor(out=aT[:, f, :], in0=sg[b][:], in1=pu[b][:, 0:GW], op=ALU.mult),
                             reads=[r_sg[b], r_pu[b]], writes=[r_aT])
                        yield

                def d_down(gi, i):
                    s = gi % 2
                    t = gi * G + i
                    rows = slice(t * 128, (t + 1) * 128)
                    for n in range(2):
                        for f in range(NFF):
                            P.op("pe", lambda e, n=n, f=f: e.matmul(
                                out=pf[:, n * 512:(n + 1) * 512], lhsT=aT[:, f, i * 128:(i + 1) * 128],
                                rhs=w_d[:, f, n * 512:(n + 1) * 512], start=(f == 0), stop=(f == NFF - 1)),
                                reads=[r_aT], writes=[r_pfh[n]])
                        yield
                    for n in range(2):
                        P.op("dve", lambda e, n=n: e.tensor_copy(out=tmp[:, n * 512:(n + 1) * 512], in_=pf[:, n * 512:(n + 1) * 512]),
                             reads=[r_pfh[n]], writes=[r_tmp])
                    yield
                    P.op("act", lambda e: e.activation(out=junk[:], in_=tmp[:], func=AF.Square, accum_out=ssE[i][:]),
                         reads=[r_tmp], writes=[r_junk, r_ssE[i]])
                    yield
                    P.op("act", lambda e: e.activation(out=rsE[i][:], in_=ssE[i][:], func=AF.Ln, scale=1.0 / D, bias=epsb[:, 0:1]),
                         reads=[r_ssE[i]], writes=[r_rsE[i]])
                    yield
                    P.op("act", lambda e: e.activation(out=rsE[i][:], in_=rsE[i][:], func=AF.Exp, scale=-0.5), reads=[r_rsE[i]], writes=[r_rsE[i]])
                    yield
                    P.op("dve", lambda e: e.scalar_tensor_tensor(out=tmp[:], in0=tmp[:], scalar=rsE[i][:, 0:1], in1=gpo[:],
                                                                 op0=ALU.mult, op1=ALU.mult),
                         reads=[r_tmp, r_rsE[i], r_gpo], writes=[r_tmp])
                    yield
                    P.op("pool", lambda e: e.tensor_tensor(out=xs[s][:, i, :], in0=xs[s][:, i, :], in1=tmp[:], op=ALU.add),
                         reads=[r_xs[s][i], r_tmp], writes=[r_xs[s][i]])
                    yield
                    P.dma("sp", out[rows, :], xs[s][:, i, :], reads=[r_xs[s][i]], chan=cxo[s][i])
                    yield

                def seq(*gens):
                    for g in gens:
                        yield from g

                def delayed(n, gen):
                    for _ in range(n):
                        yield
                    yield from gen

                rr([d_load(0)])
                rr([d_front_a(0), d_load(1)])
                rr([d_front_b(0)])
                for gi in range(NG):
                    gens = [d_gateup(gi)]
                    if gi + 1 < NG:
                        gens.append(delayed(6, d_front_a(gi + 1)))
                    rr(gens)
                    rr([d_down(gi, 0)])
                    if gi + 1 < NG:
                        rr([d_front_b(gi + 1)])
                    rr([d_down(gi, 1)])
                    if gi + 2 < NG:
                        rr([d_load(gi + 2)])
                P.run_block("D")

        phase_setup()
        src = T["x"]
        for l in range(n_layers):
            if not layer(l, src):
                break
            src = out
    return nc


_CONSTS = None


def kernel(**inputs):
    global _CONSTS
    if _CONSTS is None:
        _CONSTS = make_consts()
    nc = build()
    x = np.ascontiguousarray(inputs["x"], dtype=np.float32)
    in_maps = []
    for b in range(8):
        m = {"x": x[b]}
        for k in PARAM_SHAPES:
            m[k] = np.ascontiguousarray(inputs[k], dtype=np.float32)
        m.update(_CONSTS)
        in_maps.append(m)
    res = run_bass_kernel_spmd(nc, in_maps, core_ids=list(range(8)))
    return np.stack([r["out"] for r in res.results], 0)
```
